# Optimizing a Trainium2 kernel written in Bass

```python
import math
import jax, jax.numpy as jnp
from jax import lax
import numpy as np

D_MODEL = 4096
BATCH = 32
SEQ = 256
DEPTH = 2
DEC_BATCH = 2
DEC_SEQ = 4096
PAST_LEN = 512

GRID_W = 64
CHUNK = 128
BRANCH_W = 2048
RET_HEADS = 8
RET_DK = 128
RET_DV = BRANCH_W // RET_HEADS
MLP_GROUPS = 8
MLP_WIDTH = BRANCH_W
MLP_GW = MLP_WIDTH // MLP_GROUPS
DIFF_HEADS = 8
DIFF_DH = 128
DIFF_DV = BRANCH_W // DIFF_HEADS
FFN_HIDDEN = -(-8 * D_MODEL // (3 * 256)) * 256
ROPE_BASE = 10000.0
EPS = 1e-6
IN_SPLITS = (RET_HEADS * RET_DK, RET_HEADS * RET_DK, RET_HEADS * RET_DV, RET_HEADS * RET_DV,
             MLP_WIDTH, MLP_WIDTH,
             DIFF_HEADS * 2 * DIFF_DH, DIFF_HEADS * 2 * DIFF_DH, DIFF_HEADS * DIFF_DV,
             3 * D_MODEL)
IN_WIDTH = sum(IN_SPLITS)

kernel_name = 'hybrid_retention_gmlp_diffattn_dit_step'


def rmsnorm(x, g):
    xf = x.astype(jnp.float32)
    y = xf * lax.rsqrt(jnp.mean(xf * xf, axis=-1, keepdims=True) + EPS)
    return (y * g.astype(jnp.float32)).astype(x.dtype)


def head_layernorm(x):
    xf = x.astype(jnp.float32)
    mu = jnp.mean(xf, axis=-1, keepdims=True)
    xc = xf - mu
    return xc * lax.rsqrt(jnp.mean(xc * xc, axis=-1, keepdims=True) + EPS)


def split_cols(z):
    idx = np.cumsum(IN_SPLITS)[:-1].tolist()
    return jnp.split(z, idx, axis=-1)


def axial_rope_tables(n_tok, head_dim):
    n_rows = n_tok // GRID_W
    rows = jnp.repeat(jnp.arange(n_rows, dtype=jnp.float32), GRID_W)
    cols = jnp.tile(jnp.arange(GRID_W, dtype=jnp.float32), n_rows)
    axis_dim = head_dim // 2
    inv = ROPE_BASE ** (-jnp.arange(0, axis_dim, 2, dtype=jnp.float32) / axis_dim)
    ang = jnp.stack([rows[:, None] * inv, cols[:, None] * inv], axis=1)
    return jnp.cos(ang), jnp.sin(ang)


def apply_axial_rope(x, cos, sin):
    shp = x.shape
    axis_dim = shp[-1] // 2
    xr = x.astype(jnp.float32).reshape(shp[:-1] + (2, axis_dim))
    x1, x2 = xr[..., : axis_dim // 2], xr[..., axis_dim // 2:]
    bshape = (1, shp[1]) + (1,) * (len(shp) - 3) + cos.shape[1:]
    cb, sb = cos.reshape(bshape), sin.reshape(bshape)
    out = jnp.concatenate([x1 * cb - x2 * sb, x2 * cb + x1 * sb], axis=-1)
    return out.reshape(shp).astype(x.dtype)


def retention_scan(q, k, v, log_g, s0):
    B, S, H, _ = q.shape
    DV = v.shape[-1]
    n = S // CHUNK

    def chunks(t):
        return t.astype(jnp.float32).reshape(B, n, CHUNK, H, t.shape[-1]).transpose(1, 0, 3, 2, 4)

    qc, kc, vc = chunks(q), chunks(k), chunks(v)
    pos = jnp.arange(CHUNK, dtype=jnp.float32)
    rel = pos[:, None] - pos[None, :]
    intra_decay = jnp.where(rel >= 0, jnp.exp(log_g[:, None, None] * jnp.maximum(rel, 0.0)), 0.0)
    read_decay = jnp.exp(log_g[:, None] * (pos + 1.0))[None, :, :, None]
    write_decay = jnp.exp(log_g[:, None] * (CHUNK - 1.0 - pos))[None, :, :, None]
    chunk_decay = jnp.exp(log_g * CHUNK)[None, :, None, None]

    def step(state, inp):
        qi, ki, vi = inp
        att = jnp.einsum('bhqd,bhkd->bhqk', qi, ki) * intra_decay
        o = jnp.einsum('bhqk,bhkv->bhqv', att, vi) + jnp.einsum('bhqd,bhdv->bhqv', qi, state) * read_decay
        state = state * chunk_decay + jnp.einsum('bhkd,bhkv->bhdv', ki * write_decay, vi)
        return state, o

    state, o = lax.scan(step, s0.astype(jnp.float32), (qc, kc, vc))
    o = o.transpose(1, 0, 3, 2, 4).reshape(B, S, H, DV)
    return o, state


def retention_branch(q, k, v, g, log_g, s0_f, s0_b):
    B, S, H, _ = q.shape
    o_f, s_f = retention_scan(q, k, v, log_g[0], s0_f)
    o_b, s_b = retention_scan(jnp.flip(q, 1), jnp.flip(k, 1), jnp.flip(v, 1), log_g[1], s0_b)
    o = head_layernorm(o_f + jnp.flip(o_b, 1)).reshape(B, S, H * v.shape[-1])
    out = (jax.nn.silu(g.astype(jnp.float32)) * o).astype(q.dtype)
    return out, s_f, s_b


def chunk_mlp_branch(u, v, norm_g, ws, bs):
    B, S, _ = v.shape
    n = S // CHUNK
    u = jax.nn.gelu(u)
    v = rmsnorm(jax.nn.gelu(v), norm_g)
    vc = v.reshape(B, n, CHUNK, MLP_GROUPS, MLP_GW)
    mixed = jnp.einsum('gpq,bnqgc->bnpgc', ws, vc) + bs.T[None, None, :, :, None]
    return u * mixed.reshape(B, S, MLP_WIDTH)


def diff_attention(q, k, v, lam_vec, subln_g, lam_init):
    B, Sq, H = q.shape[0], q.shape[1], q.shape[2]
    n = Sq // CHUNK
    lv = lam_vec.astype(jnp.float32)
    lam = jnp.exp(jnp.sum(lv[0] * lv[1])) - jnp.exp(jnp.sum(lv[2] * lv[3])) + lam_init
    scale = DIFF_DH ** -0.5
    qb = q.reshape(B, n, CHUNK, H, 2, DIFF_DH).swapaxes(0, 1)

    def block(qi):
        s = jnp.einsum('bqhcd,bkhcd->bhcqk', qi, k).astype(jnp.float32) * scale
        p = jax.nn.softmax(s, axis=-1)
        a = p[:, :, 0] - lam * p[:, :, 1]
        return jnp.einsum('bhqk,bkhv->bqhv', a.astype(v.dtype), v)

    o = lax.map(block, qb).swapaxes(0, 1).reshape(B, Sq, H, DIFF_DV)
    o = rmsnorm(o, subln_g) * (1.0 - lam_init)
    return o.reshape(B, Sq, H * DIFF_DV)


def trunk_layer(x, cond, layer_idx, p, ctx):
    B, S, _ = x.shape
    mod = (jax.nn.silu(cond) @ p['w_mod'] + p['b_mod']).reshape(cond.shape[0], 1, 6, D_MODEL)
    shift1, scale1, gate1 = mod[:, :, 0], mod[:, :, 1], mod[:, :, 2]
    shift2, scale2, gate2 = mod[:, :, 3], mod[:, :, 4], mod[:, :, 5]

    h = rmsnorm(x, p['norm_g'][0]) * (1.0 + scale1) + shift1
    rq, rk, rv, rg, mu, mv, dq, dk, dv, gates = split_cols(h @ p['w_in'])
    rq = rq.reshape(B, S, RET_HEADS, RET_DK)
    rk = rk.reshape(B, S, RET_HEADS, RET_DK)
    rv = rv.reshape(B, S, RET_HEADS, RET_DV)
    dq = dq.reshape(B, S, DIFF_HEADS, 2, DIFF_DH)
    dk = dk.reshape(B, S, DIFF_HEADS, 2, DIFF_DH)
    dv = dv.reshape(B, S, DIFF_HEADS, DIFF_DV)
    log_g = jax.nn.log_sigmoid(p['ret_decay_logit'].astype(jnp.float32))
    lam_init = 0.8 - 0.6 * math.exp(-0.3 * layer_idx)

    if ctx is None:
        zero = jnp.zeros((B, RET_HEADS, RET_DK, RET_DV), jnp.float32)
        ret_o, s_f, s_b = retention_branch(rq, rk * RET_DK ** -0.5, rv, rg, log_g, zero, zero)
        diff_o = diff_attention(dq, dk, dv, p['diff_lambda'], p['diff_subln_g'], lam_init)
        ctx_out = (dk, dv, jnp.stack([s_f, s_b], axis=1).astype(x.dtype))
    else:
        cache_k_l, cache_v_l, state_l = ctx
        cos_r, sin_r = axial_rope_tables(S, RET_DK)
        cos_d, sin_d = axial_rope_tables(S, DIFF_DH)
        rq = apply_axial_rope(rq, cos_r, sin_r)
        rk = apply_axial_rope(rk, cos_r, sin_r)
        dq = apply_axial_rope(dq, cos_d, sin_d)
        dk = apply_axial_rope(dk, cos_d, sin_d)
        ret_o, _, _ = retention_branch(rq, rk * RET_DK ** -0.5, rv, rg, log_g, state_l[:, 0], state_l[:, 1])
        k_all = jnp.concatenate([cache_k_l.astype(dk.dtype), dk], axis=1)
        v_all = jnp.concatenate([cache_v_l.astype(dv.dtype), dv], axis=1)
        diff_o = diff_attention(dq, k_all, v_all, p['diff_lambda'], p['diff_subln_g'], lam_init)
        ctx_out = None

    mlp_o = chunk_mlp_branch(mu, mv, p['mlp_norm_g'], p['mlp_ws'], p['mlp_bs'])
    g_ret, g_mlp, g_diff = jnp.split(jax.nn.sigmoid(gates), 3, axis=-1)
    merged = (g_ret * (ret_o @ p['w_branch'][0]) + g_mlp * (mlp_o @ p['w_branch'][1])
              + g_diff * (diff_o @ p['w_branch'][2]))
    x = x + gate1 * rmsnorm(merged @ p['w_o'], p['norm_g'][1])

    h = rmsnorm(x, p['norm_g'][2]) * (1.0 + scale2) + shift2
    a, b = jnp.split(h @ p['w_up'], 2, axis=-1)
    f = (jax.nn.silu(a) * b) @ p['w_down']
    x = x + gate2 * rmsnorm(f, p['norm_g'][3])
    return x, ctx_out


def setup_inputs(seed: int = 0) -> dict:
    key = jax.random.key(seed)
    ks = jax.random.split(key, 21)
    f32 = jnp.float32

    def nrm(k, shape, s):
        return jax.random.normal(k, shape, f32) * s

    gam = 1.0 - 2.0 ** (-5.0 - jnp.arange(RET_HEADS, dtype=f32))
    base_logit = jnp.log(gam) - jnp.log1p(-gam)
    return {
        'x_prompt': nrm(ks[0], (BATCH, SEQ, D_MODEL), 1.0),
        'x_sample': nrm(ks[1], (DEC_BATCH, DEC_SEQ, D_MODEL), 1.0),
        'cache_k': nrm(ks[2], (DEC_BATCH, DEPTH, PAST_LEN, DIFF_HEADS, 2, DIFF_DH), 1.0),
        'cache_v': nrm(ks[3], (DEC_BATCH, DEPTH, PAST_LEN, DIFF_HEADS, DIFF_DV), 1.0),
        'state_ret': nrm(ks[4], (DEC_BATCH, DEPTH, 2, RET_HEADS, RET_DK, RET_DV), RET_DK ** -0.5),
        'c': nrm(ks[5], (DEC_BATCH, D_MODEL), 1.0),
        'c_ctx': nrm(ks[6], (D_MODEL,), 1.0),
        'w_mod': nrm(ks[7], (DEPTH, D_MODEL, 6 * D_MODEL), 0.5 * D_MODEL ** -0.5),
        'b_mod': nrm(ks[8], (DEPTH, 6 * D_MODEL), 0.01),
        'norm_g': 1.0 + nrm(ks[9], (DEPTH, 4, D_MODEL), 0.05),
        'w_in': nrm(ks[10], (DEPTH, D_MODEL, IN_WIDTH), D_MODEL ** -0.5),
        'ret_decay_logit': base_logit + nrm(ks[11], (DEPTH, 2, RET_HEADS), 0.1),
        'mlp_norm_g': 1.0 + nrm(ks[12], (DEPTH, MLP_WIDTH), 0.05),
        'mlp_ws': nrm(ks[13], (DEPTH, MLP_GROUPS, CHUNK, CHUNK), CHUNK ** -0.5),
        'mlp_bs': 1.0 + nrm(ks[14], (DEPTH, MLP_GROUPS, CHUNK), 0.05),
        'diff_lambda': nrm(ks[15], (DEPTH, 4, DIFF_DH), 0.1),
        'diff_subln_g': 1.0 + nrm(ks[16], (DEPTH, DIFF_DV), 0.05),
        'w_branch': nrm(ks[17], (DEPTH, 3, BRANCH_W, D_MODEL), BRANCH_W ** -0.5),
        'w_o': nrm(ks[18], (DEPTH, D_MODEL, D_MODEL), D_MODEL ** -0.5),
        'w_up': nrm(ks[19], (DEPTH, D_MODEL, 2 * FFN_HIDDEN), D_MODEL ** -0.5),
        'w_down': nrm(ks[20], (DEPTH, FFN_HIDDEN, D_MODEL), FFN_HIDDEN ** -0.5),
    }


def reference(x_prompt, x_sample, cache_k, cache_v, state_ret, c, c_ctx, w_mod, b_mod, norm_g, w_in,
              ret_decay_logit, mlp_norm_g, mlp_ws, mlp_bs, diff_lambda, diff_subln_g, w_branch, w_o,
              w_up, w_down):
    params = [dict(w_mod=w_mod[l], b_mod=b_mod[l], norm_g=norm_g[l], w_in=w_in[l],
                   ret_decay_logit=ret_decay_logit[l], mlp_norm_g=mlp_norm_g[l], mlp_ws=mlp_ws[l],
                   mlp_bs=mlp_bs[l], diff_lambda=diff_lambda[l], diff_subln_g=diff_subln_g[l],
                   w_branch=w_branch[l], w_o=w_o[l], w_up=w_up[l], w_down=w_down[l])
              for l in range(DEPTH)]

    h = x_prompt
    ks_out, vs_out, ss_out = [], [], []
    for l in range(DEPTH):
        h, (k_l, v_l, s_l) = trunk_layer(h, c_ctx[None, :], l, params[l], None)
        ks_out.append(k_l)
        vs_out.append(v_l)
        ss_out.append(s_l)
    y_prompt = h
    new_cache_k = jnp.stack(ks_out, axis=1)
    new_cache_v = jnp.stack(vs_out, axis=1)
    new_state_ret = jnp.stack(ss_out, axis=1)

    h = x_sample
    for l in range(DEPTH):
        h, _ = trunk_layer(h, c, l, params[l], (cache_k[:, l], cache_v[:, l], state_ret[:, l]))
    y_sample = h

    return (y_prompt, y_sample, new_cache_k, new_cache_v, new_state_ret)
```

```python
import contextlib
import numpy as np
import concourse.bass as bass
import concourse.mybir as mybir
from concourse.bass_utils import run_bass_kernel_spmd

F32 = mybir.dt.float32
BF16 = mybir.dt.bfloat16
AF = mybir.ActivationFunctionType
ALU = mybir.AluOpType
AX = mybir.AxisListType

D = 4096
KC = 32
NIN = 28672
FH = 11008
O_RQ, O_RK, O_RV, O_RG, O_MU, O_MV, O_DQ, O_DK, O_DV, O_GT = 0, 1024, 2048, 4096, 6144, 8192, 10240, 12288, 14336, 16384
EPS = 1e-6
ENGS = ["pe", "act", "dve", "pool", "sp"]


class Res:
    __slots__ = ("w", "rs")

    def __init__(self):
        self.w = None
        self.rs = []


class Op:
    __slots__ = ("eng", "fn", "waits", "sem", "val", "dma")


class Sched:
    def __init__(self, nc, es, ring_w=8):
        self.nc = nc
        self.ops = {e: [] for e in ENGS}
        self.cnt = {e: 0 for e in ENGS}
        self.dcnt = {e: 0 for e in ENGS}
        self.waited = {e: {} for e in ENGS}
        self.esem = {e: es.enter_context(nc.semaphore("es_" + e)) for e in ENGS}
        self.ring = {e: [es.enter_context(nc.semaphore("dr_%s%d" % (e, i))) for i in range(ring_w)]
                     for e in ("sp", "pool", "act")}
        self.semobj = {}

    def add(self, eng, fn, r=(), w=(), dma=False):
        deps = []
        for x in r:
            if x.w is not None:
                deps.append(x.w)
        for x in w:
            if x.w is not None:
                deps.append(x.w)
            deps.extend(x.rs)
        op = Op()
        op.eng, op.fn, op.dma = eng, fn, dma
        need = {}
        if dma:
            j = self.dcnt[eng]
            self.dcnt[eng] += 1
            ring = self.ring[eng]
            W = len(ring)
            op.sem = ring[j % W]
            op.val = 16 * (j // W + 1)
            if j >= W:
                need[id(op.sem)] = (op.sem, 16 * (j // W))
        else:
            self.cnt[eng] += 1
            op.sem = self.esem[eng]
            op.val = self.cnt[eng]
        for d in deps:
            if d.eng == "pe" and eng == "pe" and not d.dma and not dma:
                continue
            k = id(d.sem)
            if k not in need or need[k][1] < d.val:
                need[k] = (d.sem, d.val)
        waits = []
        wd = self.waited[eng]
        for k, (sem, val) in need.items():
            if wd.get(k, 0) >= val:
                continue
            wd[k] = val
            waits.append((sem, val))
        op.waits = waits
        self.ops[eng].append(op)
        for x in r:
            x.rs.append(op)
        for x in w:
            x.w = op
            x.rs = []
        return op

    def barrier(self):
        targets = []
        for e in ENGS:
            if self.cnt[e] > 0:
                targets.append((self.esem[e], self.cnt[e]))
        for e, ring in self.ring.items():
            n = self.dcnt[e]
            W = len(ring)
            for i in range(min(n, W)):
                cntj = (n - 1 - i) // W + 1
                targets.append((ring[i], 16 * cntj))
        for e in ENGS:
            op = Op()
            op.eng, op.dma = e, False
            op.fn = None
            wd = self.waited[e]
            waits = []
            for sem, val in targets:
                if sem is self.esem[e]:
                    continue
                if wd.get(id(sem), 0) >= val:
                    continue
                wd[id(sem)] = val
                waits.append((sem, val))
            op.waits = waits
            op.sem = None
            op.val = 0
            self.ops[e].append(op)

    def emit(self, block):
        def run(eng, name):
            for op in self.ops[name]:
                for sem, val in op.waits:
                    eng.wait_ge(sem, val)
                if op.fn is None:
                    continue
                inst = op.fn(eng)
                inst.then_inc(op.sem, 16 if op.dma else 1)

        @block.tensor
        def _(e):
            run(e, "pe")

        @block.scalar
        def _(e):
            run(e, "act")

        @block.vector
        def _(e):
            run(e, "dve")

        @block.gpsimd
        def _(e):
            run(e, "pool")

        @block.sync
        def _(e):
            run(e, "sp")


class ZCat:
    def __init__(self, parts, rp):
        self.parts, self.rp = parts, rp

    def __getitem__(self, key):
        rs, cs = key
        p = rs.start // self.rp
        assert (rs.stop - 1) // self.rp == p
        return self.parts[p][rs.start - p * self.rp:rs.stop - p * self.rp, cs]


class Cfg:
    def __init__(self, nseq=4, layers=2, do_sample=False, stop=None, dbg=(), slen=4096):
        self.slen = slen
        self.nseq = nseq
        self.layers = layers
        self.do_sample = do_sample
        self.stop = stop
        self.dbg = dbg


def build(cfg):
    nc = bass.Bass("TRN2", target_bir_lowering=False)
    NSEQ = cfg.nseq
    NT = NSEQ * 2
    T = NT * 128

    def din(name, shape, dt=F32):
        return nc.dram_tensor(name, list(shape), dt, kind="ExternalInput").ap()

    def dout(name, shape, dt=F32):
        return nc.dram_tensor(name, list(shape), dt, kind="ExternalOutput").ap()

    def dscr(name, shape, dt=F32):
        kind = "ExternalOutput" if name in cfg.dbg else "Internal"
        return nc.dram_tensor(name, list(shape), dt, kind=kind).ap()

    xp = din("xp", [T, D])
    cond = din("cond", [2, D])
    w_mod = din("w_mod", [2, D, 6 * D])
    b_mod = din("b_mod", [2, 6 * D])
    norm_g = din("norm_g", [2, 4, D])
    w_in = din("w_in", [2, D, NIN])
    rdl = din("rdl", [2, 16])
    mlp_ng = din("mlp_ng", [2, 2048])
    mlp_ws = din("mlp_ws", [2, 8, 128, 128])
    mlp_bs = din("mlp_bs", [2, 8, 128])
    dlam = din("dlam", [2, 512])
    dsub = din("dsub", [2, 256])
    w_br = din("w_br", [2, 3, 2048, D])
    w_o = din("w_o", [2, D, D])
    w_up = din("w_up", [2, D, 2 * FH])
    w_dn = din("w_dn", [2, FH, D])
    ctab = din("ctab", [128, 128 + 512 + 4])

    yp = dout("yp", [T, D])
    nck = dout("nck", [NSEQ, 2, 256, 2048])
    ncv = dout("ncv", [NSEQ, 2, 256, 2048])
    nst = dout("nst", [NSEQ, 2, 2, 8, 128, 256])

    modD = dscr("modD", [2, 2, 6 * D])
    zD = dscr("zD", [T, NIN])
    xD = dscr("xD", [T, D])
    boT = dscr("boT", [3, 2048, T], BF16)
    TMAX = max(T, min(1024, cfg.slen))
    mgD = dscr("mgD", [TMAX, D])
    m2D = dscr("m2D", [TMAX, D])
    fT = dscr("fT", [FH, TMAX], BF16)

    SLEN = cfg.slen
    PAST = 512
    xs = ck = cv = sr = ropec = ropes = ys = zS = xSD = boTS = kTD = qTD = sbD = None
    if SLEN:
        xs = din("xs", [SLEN, D])
        ck = din("ck", [2, PAST, 2048])
        cv = din("cv", [2, PAST, 2048])
        sr = din("sr", [2, 2, 8, 128, 256])
        ropec = din("ropec", [SLEN, 128])
        ropes = din("ropes", [SLEN, 128])
        ys = dout("ys", [SLEN, D])
        TPS = min(1024, SLEN)
        zS = ZCat([dscr("zS%d" % p_, [TPS, NIN]) for p_ in range(SLEN // TPS)], TPS)
        xSD = dscr("xSD", [SLEN, D])
        boTS = dscr("boTS", [3, 2048, SLEN], BF16)
        kTD = dscr("kTD", [16, 128, PAST + SLEN], BF16)
        qTD = dscr("qTD", [16, 128, SLEN], BF16)
        sbD = dscr("sbD", [SLEN // 128, 128, 2048], BF16)

    es = contextlib.ExitStack()
    with es:
        def sb(name, shape, dt):
            return es.enter_context(nc.sbuf_tensor(name, list(shape), dt))

        BIGA = sb("BIGA", [128, 32768], BF16)
        BIGW = sb("BIGW", [128, 32768], BF16)
        FT = sb("FT", [128, 12288], F32)
        ST = sb("ST", [128, 2048], F32)
        CT = sb("CT", [128, 128 + 512 + 4], F32)
        IDB = sb("IDB", [128, 128], BF16)
        COLS = sb("COLS", [128, 2 * 4 * 32], F32)
        SM = sb("SM", [128, 512], F32)
        PS = es.enter_context(nc.psum_tensor("PS", [128, 8, 512], F32))

        sch = Sched(nc, es)
        A = sch.add
        ident = CT[:, 0:128]
        bank_res = [Res() for _ in range(8)]
        st_res = [Res() for _ in range(4)]
        sm_res = Res()

        def psb(b):
            return PS[:, b, :]

        r_ct = Res()
        A("sp", lambda e: e.dma_start(out=CT[:, :], in_=ctab[:, :]), w=[r_ct], dma=True)
        r_idb = Res()
        A("dve", lambda e: e.tensor_copy(out=IDB[:, :], in_=CT[:, 0:128]), r=[r_ct], w=[r_idb])
        sch.barrier()

        actT = BIGA[:, :].rearrange("p (k t) -> p k t", k=32)
        actT_res = [Res() for _ in range(8)]
        wslot = [BIGW[:, s * 8192:(s + 1) * 8192].rearrange("p (k c) -> p k c", k=16) for s in range(4)]
        wslot_res = [Res() for _ in range(4)]
        state = {"panel": 0, "gb": 0, "evac": 0, "st": 0}

        def gemm(nkc, act_fn, wsrc_fn, npanels, ntiles, m_rows, evac_fn, act_res_fn, gbanks=(0, 1, 2)):
            spp = (nkc + 15) // 16
            for n in range(npanels):
                slots = []
                for s in range(spp):
                    sl = ((state["panel"] % 2) * 2 + s) if spp == 2 else state["panel"] % 4
                    slots.append(sl)
                    k0 = s * 16
                    nk = min(16, nkc - k0)
                    for (c0, c1, src) in wsrc_fn(n, k0, nk):
                        A("pool", (lambda e, sl=sl, nk=nk, c0=c0, c1=c1, src=src:
                                   e.dma_start(out=wslot[sl][:, 0:nk, c0:c1], in_=src)),
                          w=[wslot_res[sl]], dma=True)
                state["panel"] += 1
                for t in range(ntiles):
                    b = gbanks[state["gb"] % len(gbanks)]
                    state["gb"] += 1

                    def mm(e, t=t, b=b, slots=slots, n=n):
                        inst = None
                        for kc in range(nkc):
                            inst = e.matmul(PS[0:m_rows, b, :], act_fn(kc, t, n), wslot[slots[kc // 16]][:, kc % 16, :],
                                            start=(kc == 0), stop=(kc == nkc - 1))
                        return inst
                    A("pe", mm, r=[act_res_fn(t, n)] + [wslot_res[s] for s in slots], w=[bank_res[b]])
                    evac_fn(t, n, b)

        def next_st():
            i = state["st"] % 4
            state["st"] += 1
            return i

        def evac_eng():
            state["evac"] += 1
            return "act" if state["evac"] % 2 == 0 else "dve"

        def copy_op(eng_name, out, in_):
            if eng_name == "act":
                return lambda e: e.activation(out=out, in_=in_, func=AF.Copy)
            return lambda e: e.tensor_copy(out=out, in_=in_)

        def mod_phase(l):
            condf = SM[:, 0:64].rearrange("p (k r) -> p k r", r=2)
            r_c = Res()
            for r_ in range(2):
                def ld(e, r_=r_):
                    with nc.allow_non_contiguous_dma(reason="tiny column-layout load"):
                        return e.dma_start(out=condf[:, :, r_], in_=cond[r_].rearrange("(k p) -> p k", p=128))
                A("sp", ld, w=[r_c], dma=True)
            sT = BIGA[:, 0:64].rearrange("p (k r) -> p k r", r=2)
            r_s = Res()
            A("act", lambda e: e.activation(out=sT, in_=condf, func=AF.Silu), r=[r_c], w=[r_s])

            def wsrc(n, k0, nk):
                return [(0, 512, w_mod[l, k0 * 128:(k0 + nk) * 128, n * 512:(n + 1) * 512]
                         .rearrange("(k p) c -> p k c", p=128))]

            def evac(t, n, b):
                i = next_st()
                bt = ST[0:2, i * 512:(i + 1) * 512]
                A("sp", lambda e: e.dma_start(out=bt, in_=b_mod[l, n * 512:(n + 1) * 512].partition_broadcast(2)),
                  w=[st_res[i]], dma=True)
                A("dve", lambda e: e.tensor_tensor(out=bt, in0=bt, in1=PS[0:2, b, :], op=ALU.add),
                  r=[bank_res[b]], w=[st_res[i]])
                A("sp", lambda e: e.dma_start(out=modD[l, :, n * 512:(n + 1) * 512], in_=bt), r=[st_res[i]], dma=True)

            gemm(32, lambda kc, t, n: sT[:, kc, :], wsrc, 48, 1, 2, evac, lambda t, n: r_s)
            sch.barrier()
            tmp = SM[:, 64:64 + 6 * 32].rearrange("p (v k) -> p v k", v=6)
            r_t = Res()
            for gi in range(2):
                loads = [(0, norm_g[l, 0]), (1, norm_g[l, 2]), (2, modD[l, gi, 0:D]), (3, modD[l, gi, D:2 * D]),
                         (4, modD[l, gi, 3 * D:4 * D]), (5, modD[l, gi, 4 * D:5 * D])]
                for (vi, src) in loads:
                    def ld(e, vi=vi, src=src):
                        with nc.allow_non_contiguous_dma(reason="tiny column-layout load"):
                            return e.dma_start(out=tmp[:, vi, :], in_=src.rearrange("(k p) -> p k", p=128))
                    A("sp", ld, w=[r_t], dma=True)
                cg = COLS[:, gi * 128:(gi + 1) * 128].rearrange("p (v k) -> p v k", v=4)
                r_cols = Res()
                A("dve", lambda e, cg=cg: e.scalar_tensor_tensor(out=cg[:, 0, :], in0=tmp[:, 3, :], scalar=1.0,
                                                                  in1=tmp[:, 0, :], op0=ALU.add, op1=ALU.mult),
                  r=[r_t], w=[r_cols])
                A("dve", lambda e, cg=cg: e.tensor_copy(out=cg[:, 1, :], in_=tmp[:, 2, :]), r=[r_t], w=[r_cols])
                A("dve", lambda e, cg=cg: e.scalar_tensor_tensor(out=cg[:, 2, :], in0=tmp[:, 5, :], scalar=1.0,
                                                                  in1=tmp[:, 1, :], op0=ALU.add, op1=ALU.mult),
                  r=[r_t], w=[r_cols])
                A("dve", lambda e, cg=cg: e.tensor_copy(out=cg[:, 3, :], in_=tmp[:, 4, :]), r=[r_t], w=[r_cols])
                sch.barrier()

        X = {'NT': NT, 'Z': zD, 'BO': boT, 'XW': xD}

        adaln_junk_res = Res()
        def adaln_tile(gi, which, t, xt, r_x):
            cg = COLS[:, gi * 128:(gi + 1) * 128].rearrange("p (v k) -> p v k", v=4)
            Ac, Bc = cg[:, 2 * which, :], cg[:, 2 * which + 1, :]
            junk = BIGW[:, 0:4096]
            ss = SM[:, 300 + t:301 + t]
            r_ss = Res()
            r_junk = adaln_junk_res
            A("act", lambda e: e.activation(out=junk, in_=xt, func=AF.Square, accum_out=ss), r=[r_x], w=[r_junk, r_ss])
            A("dve", lambda e: e.tensor_scalar(out=ss, in0=ss, scalar1=1.0 / D, scalar2=EPS, op0=ALU.mult, op1=ALU.add),
              r=[r_ss], w=[r_ss])
            A("act", lambda e: e.activation(out=ss, in_=ss, func=AF.Sqrt), r=[r_ss], w=[r_ss])
            A("dve", lambda e: e.reciprocal(out=ss, in_=ss), r=[r_ss], w=[r_ss])
            A("pool", lambda e: e.tensor_scalar(out=xt, in0=xt, scalar1=ss, scalar2=None, op0=ALU.mult),
              r=[r_ss, r_x], w=[r_x])
            for q in range(8):
                b = 4 + (state["gb"] % 4)
                state["gb"] += 1

                def tr(e, q=q, b=b):
                    inst = None
                    for j in range(4):
                        c = q * 4 + j
                        inst = e.transpose(PS[:, b, j * 128:(j + 1) * 128], xt[:, c * 128:(c + 1) * 128], ident)
                    return inst
                A("pe", tr, r=[r_x, r_ct], w=[bank_res[b]])
                for j in range(4):
                    c = q * 4 + j
                    o = actT[:, c, t * 128:(t + 1) * 128]
                    i_ = PS[:, b, j * 128:(j + 1) * 128]
                    if j % 2 == 0:
                        A("act", lambda e, o=o, i_=i_, c=c: e.activation(out=o, in_=i_, func=AF.Identity,
                                                                       scale=Ac[:, c:c + 1], bias=Bc[:, c:c + 1]),
                          r=[bank_res[b]], w=[actT_res[t]])
                    else:
                        A("dve", lambda e, o=o, i_=i_, c=c: e.tensor_scalar(out=o, in0=i_, scalar1=Ac[:, c:c + 1],
                                                                           scalar2=Bc[:, c:c + 1], op0=ALU.mult,
                                                                           op1=ALU.add),
                          r=[bank_res[b]], w=[actT_res[t]])

        def p1_phase(gi, l, xsrc):
            Z_, BO_, XW_, NT_ = X['Z'], X['BO'], X['XW'], X['NT']
            xres = [Res(), Res()]
            for t in range(NT_):
                xt = FT[:, (t % 2) * 4096:(t % 2 + 1) * 4096]
                A("sp", lambda e, xt=xt, t=t: e.dma_start(out=xt, in_=xsrc[t * 128:(t + 1) * 128, :]),
                  w=[xres[t % 2]], dma=True)
                adaln_tile(gi, 0, t, xt, xres[t % 2])

        def g1_phase(gi, l):
            Z_, BO_, XW_, NT_ = X['Z'], X['BO'], X['XW'], X['NT']
            def wsrc(n, k0, nk):
                return [(0, 512, w_in[l, k0 * 128:(k0 + nk) * 128, n * 512:(n + 1) * 512]
                         .rearrange("(k p) c -> p k c", p=128))]

            def evac(t, n, b):
                i = next_st()
                st = ST[:, i * 512:(i + 1) * 512]
                en = evac_eng()
                A(en, copy_op(en, st, psb(b)), r=[bank_res[b]], w=[st_res[i]])
                A("sp", lambda e: e.dma_start(out=Z_[t * 128:(t + 1) * 128, n * 512:(n + 1) * 512], in_=st),
                  r=[st_res[i]], dma=True)
                c0 = n * 512
                if gi == 0 and O_DK <= c0 < O_DV:
                    A("sp", lambda e: e.dma_start(out=nck[t // 2, l, (t % 2) * 128:(t % 2 + 1) * 128,
                                                         c0 - O_DK:c0 - O_DK + 512], in_=st),
                      r=[st_res[i]], dma=True)
                if gi == 0 and O_DV <= c0 < O_GT:
                    A("sp", lambda e: e.dma_start(out=ncv[t // 2, l, (t % 2) * 128:(t % 2 + 1) * 128,
                                                         c0 - O_DV:c0 - O_DV + 512], in_=st),
                      r=[st_res[i]], dma=True)

            gemm(32, lambda kc, t, n: actT[:, kc, t * 128:(t + 1) * 128], wsrc, NIN // 512, NT_, 128, evac,
                 lambda t, n: actT_res[t])


        RS = {}

        def R_(name):
            if name not in RS:
                RS[name] = Res()
            return RS[name]

        bctr = {"b": 0}

        def nb():
            b = bctr["b"] % 8
            bctr["b"] += 1
            return b

        def PB(b):
            return PS[:, b, :].bitcast(BF16)

        MASK = sb("MASK", [128, 1024], F32)
        SM2 = sb("SM2", [128, 512], F32)
        SUBB = sb("SUBB", [128, 256], F32)

        def mix_phase(gi, l, mode="P"):
            Z_, BO_, XW_, NT_ = X['Z'], X['BO'], X['XW'], X['NT']
            lam_init = 0.8 - 0.6 * float(np.exp(-0.3 * l))
            s_dk = 128.0 ** -0.5
            a_sc = 128.0 ** -0.5
            Vb = [BIGA[:, c * 2048:(c + 1) * 2048] for c in range(2)]
            QV = [BIGA[:, 4096 + v * 1024:4096 + (v + 1) * 1024] for v in range(6)]
            QT = [[BIGA[:, 10240 + (c * 4 + v) * 1024:10240 + (c * 4 + v + 1) * 1024].rearrange("p (h t) -> p h t", h=8)
                   for v in range(4)] for c in range(2)]
            ATT = BIGA[:, 18432:19456].rearrange("p (h t) -> p h t", h=8)
            UB = [BIGA[:, 19456 + i * 2048:19456 + (i + 1) * 2048].rearrange("p (h v) -> p h v", h=8) for i in range(2)]
            RETO = BIGA[:, 23552:25600]
            MGB = BIGA[:, 25600:29696].bitcast(F32)
            WST = BIGA[:, 29696:30720].rearrange("p (g t) -> p g t", g=8)
            DQb = [BIGW[:, c * 2048:(c + 1) * 2048] for c in range(2)]
            DKb = [BIGW[:, 4096 + c * 2048:4096 + (c + 1) * 2048] for c in range(2)]
            DVb = [BIGW[:, 8192 + c * 2048:8192 + (c + 1) * 2048] for c in range(2)]
            DQT = [BIGW[:, 12288 + c * 2048:12288 + (c + 1) * 2048].rearrange("p (j t) -> p j t", j=16) for c in range(2)]
            DKT = BIGW[:, 16384:20480].rearrange("p (j t) -> p j t", j=16)
            AB = BIGW[:, 20480:20736]
            AT = BIGW[:, 20736:20992].rearrange("p (k t) -> p k t", k=2)
            DIFO = BIGW[:, 20992:23040]
            MLPO = BIGW[:, 23040:25088]
            VN = BIGW[:, 25088:27136]
            BOS = BIGW[:, 27136:29184].rearrange("p (k t) -> p k t", k=16)
            FB = [FT[:, i * 2048:(i + 1) * 2048] for i in range(6)]
            EX = ST[:, 0:512].rearrange("p (j k) -> p j k", j=2)

            lg = SM2[:, 0:16]
            A("sp", lambda e: e.dma_start(out=lg, in_=rdl[l].partition_broadcast(128)), w=[R_("lg")], dma=True)
            A("act", lambda e: e.activation(out=lg, in_=lg, func=AF.Exp, scale=-1.0), r=[R_("lg")], w=[R_("lg")])
            A("dve", lambda e: e.tensor_scalar(out=lg, in0=lg, scalar1=1.0, scalar2=None, op0=ALU.add),
              r=[R_("lg")], w=[R_("lg")])
            A("act", lambda e: e.activation(out=lg, in_=lg, func=AF.Ln), r=[R_("lg")], w=[R_("lg")])
            A("dve", lambda e: e.tensor_scalar(out=lg, in0=lg, scalar1=-1.0, scalar2=None, op0=ALU.mult),
              r=[R_("lg")], w=[R_("lg")])
            DEC = SM2[:, 16:48]
            CD = SM2[:, 48:64]
            for v, (lo, pc) in enumerate([(0, 0), (8, 1), (0, 2), (8, 3)]):
                A("dve", lambda e, v=v, lo=lo, pc=pc: e.tensor_scalar(out=DEC[:, v * 8:(v + 1) * 8], in0=lg[:, lo:lo + 8],
                                                                      scalar1=CT[:, 640 + pc:641 + pc], scalar2=None,
                                                                      op0=ALU.mult),
                  r=[R_("lg"), r_ct], w=[R_("dec")])
            A("act", lambda e: e.activation(out=DEC, in_=DEC, func=AF.Exp), r=[R_("dec")], w=[R_("dec")])
            A("dve", lambda e: e.tensor_scalar(out=DEC[:, 16:32], in0=DEC[:, 16:32], scalar1=s_dk, scalar2=None,
                                               op0=ALU.mult), r=[R_("dec")], w=[R_("dec")])
            A("act", lambda e: e.activation(out=CD, in_=lg, func=AF.Exp, scale=128.0), r=[R_("lg")], w=[R_("cd")])
            MK = MASK[:, :].rearrange("p (h q) -> p h q", h=8)
            Ef, Lf, Eb, Lb = CT[:, 128:256], CT[:, 256:384], CT[:, 384:512], CT[:, 512:640]
            t1, t2 = FB[4][:, 0:128], FB[4][:, 128:256]
            for h in range(8):
                A("act", lambda e, h=h: e.activation(out=t1, in_=Ef, func=AF.Exp, scale=lg[:, h:h + 1]),
                  r=[R_("lg"), r_ct], w=[R_("fb4")])
                A("dve", lambda e: e.scalar_tensor_tensor(out=t1, in0=t1, scalar=s_dk, in1=Lf, op0=ALU.mult, op1=ALU.mult),
                  r=[R_("fb4")], w=[R_("fb4")])
                A("act", lambda e, h=h: e.activation(out=t2, in_=Eb, func=AF.Exp, scale=lg[:, 8 + h:9 + h]),
                  r=[R_("lg"), r_ct], w=[R_("fb4b")])
                A("dve", lambda e: e.scalar_tensor_tensor(out=t2, in0=t2, scalar=s_dk, in1=Lb, op0=ALU.mult, op1=ALU.mult),
                  r=[R_("fb4b")], w=[R_("fb4b")])
                A("dve", lambda e, h=h: e.tensor_tensor(out=MK[:, h, :], in0=t1, in1=t2, op=ALU.add),
                  r=[R_("fb4"), R_("fb4b")], w=[R_("mask")])
            wsf = FB[5][:, 0:1024].rearrange("p (g q) -> p g q", g=8)
            A("sp", lambda e: e.dma_start(out=wsf, in_=mlp_ws[l].rearrange("g p q -> p g q")), w=[R_("fb5")], dma=True)
            for half in range(2):
                b = nb()

                def trw(e, half=half, b=b):
                    inst = None
                    for j in range(4):
                        inst = e.transpose(PS[:, b, j * 128:(j + 1) * 128], wsf[:, half * 4 + j, :], ident)
                    return inst
                A("pe", trw, r=[R_("fb5"), r_ct], w=[bank_res[b]])
                A("dve", lambda e, half=half, b=b: e.tensor_copy(
                    out=WST[:, half * 4:(half + 1) * 4, :], in_=PS[:, b, :].rearrange("p (j t) -> p j t", j=4)),
                  r=[bank_res[b]], w=[R_("wst")])
            BSC = SM2[:, 64:72]

            def ldbs(e):
                with nc.allow_non_contiguous_dma(reason="tiny column-layout load"):
                    return e.dma_start(out=BSC, in_=mlp_bs[l].rearrange("g p -> p g"))
            A("sp", ldbs, w=[R_("bsc")], dma=True)
            A("sp", lambda e: e.dma_start(out=MGB, in_=mlp_ng[l].partition_broadcast(128)), w=[R_("mgb")], dma=True)
            dl = FB[4][:, 512:1024]
            dl4 = dl.rearrange("p (a b d) -> p a b d", a=2, b=2)
            pr = FB[4][:, 1024:1280].rearrange("p (a d) -> p a d", a=2)
            A("sp", lambda e: e.dma_start(out=dl, in_=dlam[l].partition_broadcast(128)), w=[R_("dl")], dma=True)
            A("dve", lambda e: e.tensor_tensor(out=pr, in0=dl4[:, :, 0, :], in1=dl4[:, :, 1, :], op=ALU.mult),
              r=[R_("dl")], w=[R_("pr")])
            LE = SM2[:, 72:74]
            NLAM = SM2[:, 74:75]
            A("dve", lambda e: e.tensor_reduce(out=LE, in_=pr, axis=AX.X, op=ALU.add), r=[R_("pr")], w=[R_("le")])
            A("act", lambda e: e.activation(out=LE, in_=LE, func=AF.Exp), r=[R_("le")], w=[R_("le")])
            A("dve", lambda e: e.tensor_tensor(out=NLAM, in0=LE[:, 1:2], in1=LE[:, 0:1], op=ALU.subtract),
              r=[R_("le")], w=[R_("nlam")])
            A("dve", lambda e: e.tensor_scalar(out=NLAM, in0=NLAM, scalar1=-lam_init, scalar2=None, op0=ALU.add),
              r=[R_("nlam")], w=[R_("nlam")])
            A("sp", lambda e: e.dma_start(out=SUBB[:, :], in_=dsub[l].partition_broadcast(128)), w=[R_("subb")], dma=True)
            A("dve", lambda e: e.tensor_scalar(out=SUBB[:, :], in0=SUBB[:, :], scalar1=1.0 - lam_init, scalar2=None,
                                               op0=ALU.mult), r=[R_("subb")], w=[R_("subb")])

            sch.barrier()

            def rstd_from(ss, n, rname):
                A("dve", lambda e: e.tensor_scalar(out=ss, in0=ss, scalar1=1.0 / n, scalar2=EPS, op0=ALU.mult, op1=ALU.add),
                  r=[R_(rname)], w=[R_(rname)])
                A("act", lambda e: e.activation(out=ss, in_=ss, func=AF.Sqrt), r=[R_(rname)], w=[R_(rname)])
                A("dve", lambda e: e.reciprocal(out=ss, in_=ss), r=[R_(rname)], w=[R_(rname)])

            def store_branch(bi, src, srcname, t):
                for half in range(2):
                    b = nb()

                    def trb(e, half=half, b=b):
                        inst = None
                        for j in range(8):
                            c = half * 8 + j
                            inst = e.transpose(PB(b)[:, j * 128:(j + 1) * 128], src[:, c * 128:(c + 1) * 128], IDB[:, :])
                        return inst
                    A("pe", trb, r=[R_(srcname), r_idb], w=[bank_res[b]])
                    en = evac_eng()
                    A(en, copy_op(en, BOS[:, half * 8:(half + 1) * 8, :], PB(b).rearrange("p (j t) -> p j t", j=8)),
                      r=[bank_res[b]], w=[R_("bos")])
                A("sp", lambda e: e.dma_start(out=BO_[bi, :, t * 128:(t + 1) * 128].rearrange("(k p) t -> p k t", p=128),
                                              in_=BOS), r=[R_("bos")], dma=True)

            def gelu(dst, src, tmp, names):
                rr = [R_(n) for n in names]
                A("act", lambda e: e.activation(out=tmp, in_=src, func=AF.Square), r=[rr[1]], w=[rr[2]])
                A("dve", lambda e: e.tensor_scalar(out=tmp, in0=tmp, scalar1=0.044715, scalar2=1.0, op0=ALU.mult, op1=ALU.add),
                  r=[rr[2]], w=[rr[2]])
                A("pool", lambda e: e.tensor_tensor(out=tmp, in0=tmp, in1=src, op=ALU.mult), r=[rr[1], rr[2]], w=[rr[2]])
                A("act", lambda e: e.activation(out=tmp, in_=tmp, func=AF.Sigmoid, scale=1.5957691216057308),
                  r=[rr[2]], w=[rr[2]])
                A("dve", lambda e: e.tensor_tensor(out=dst, in0=tmp, in1=src, op=ALU.mult), r=[rr[1], rr[2]], w=[rr[0]])

            def gmlp_chunk(t):
                rows = slice(t * 128, (t + 1) * 128)
                MU, MV_, T1, T2 = FB[0][:, :], FB[1][:, :], FB[2][:, :], FB[3][:, :]
                A("sp", lambda e, rows=rows: e.dma_start(out=FB[0][:, :], in_=Z_[rows, O_MU:O_MV]), w=[R_("fb0")], dma=True)
                A("sp", lambda e, rows=rows: e.dma_start(out=FB[1][:, :], in_=Z_[rows, O_MV:O_DQ]), w=[R_("fb1")], dma=True)
                gelu(MU, MU, T1, ["fb0", "fb0", "fb2"])
                gelu(MV_, MV_, T2, ["fb1", "fb1", "fb3"])
                ssv = SM2[:, 200:201]
                A("act", lambda e: e.activation(out=FB[3][:, :], in_=FB[1][:, :], func=AF.Square, accum_out=ssv),
                  r=[R_("fb1")], w=[R_("fb3"), R_("ssv")])
                rstd_from(ssv, 2048.0, "ssv")
                A("dve", lambda e: e.tensor_scalar(out=VN, in0=FB[1][:, :], scalar1=ssv, scalar2=None, op0=ALU.mult),
                  r=[R_("fb1"), R_("ssv")], w=[R_("vn")])
                for gp in range(4):
                    b = nb()

                    def mmg(e, gp=gp, b=b):
                        inst = None
                        for j in range(2):
                            g = gp * 2 + j
                            inst = e.matmul(PS[:, b, j * 256:(j + 1) * 256], WST[:, g, :], VN[:, g * 256:(g + 1) * 256],
                                            start=True, stop=True)
                        return inst
                    A("pe", mmg, r=[R_("wst"), R_("vn")], w=[bank_res[b]])
                    tg = FB[2][:, gp * 512:(gp + 1) * 512]
                    A("dve", lambda e, gp=gp, b=b, tg=tg: e.tensor_tensor(out=tg, in0=PS[:, b, :],
                                                                        in1=MGB[:, gp * 512:(gp + 1) * 512], op=ALU.mult),
                      r=[bank_res[b], R_("mgb")], w=[R_("fb2")])
                    for j in range(2):
                        g = gp * 2 + j
                        A("dve", lambda e, g=g: e.scalar_tensor_tensor(
                            out=MLPO[:, g * 256:(g + 1) * 256], in0=FB[2][:, g * 256:(g + 1) * 256], scalar=BSC[:, g:g + 1],
                            in1=FB[0][:, g * 256:(g + 1) * 256], op0=ALU.add, op1=ALU.mult),
                          r=[R_("fb2"), R_("fb0"), R_("bsc")], w=[R_("mlpo")])
                store_branch(1, MLPO, "mlpo", t)

            for s in range(NSEQ if mode == "P" else 0):
                for c in range(2):
                    t = 2 * s + c
                    rows = slice(t * 128, (t + 1) * 128)
                    rqk = FB[c][:, :]
                    A("sp", lambda e, rqk=rqk, rows=rows: e.dma_start(out=rqk, in_=Z_[rows, O_RQ:O_RV]),
                      w=[R_("fb%d" % c)], dma=True)
                    A("sp", lambda e, c=c, rows=rows: e.dma_start(out=FB[2 + c][:, :], in_=Z_[rows, O_RG:O_MU]),
                      w=[R_("fb%d" % (2 + c))], dma=True)
                    A("pool", lambda e, c=c, rows=rows: e.dma_start(out=Vb[c], in_=Z_[rows, O_RV:O_RG]),
                      w=[R_("vb%d" % c)], dma=True)
                    A("act", lambda e, c=c: e.activation(out=FB[2 + c][:, :], in_=FB[2 + c][:, :], func=AF.Silu),
                      r=[R_("fb%d" % (2 + c))], w=[R_("fb%d" % (2 + c))])
                    q3 = rqk[:, 0:1024].rearrange("p (h d) -> p h d", h=8)
                    k3 = rqk[:, 1024:2048].rearrange("p (h d) -> p h d", h=8)
                    A("dve", lambda e, rqk=rqk: e.tensor_copy(out=QV[0], in_=rqk[:, 0:1024]), r=[R_("fb%d" % c)], w=[R_("qv0")])
                    A("pool", lambda e, rqk=rqk: e.tensor_copy(out=QV[3], in_=rqk[:, 1024:2048]), r=[R_("fb%d" % c)], w=[R_("qv3")])
                    for v, (src3, d0) in enumerate([(q3, 0), (q3, 8), (k3, 16), (k3, 24)]):
                        vi = [1, 2, 4, 5][v]
                        en = "dve" if v % 2 == 0 else "pool"
                        A(en, lambda e, vi=vi, src3=src3, d0=d0: e.tensor_tensor(
                            out=QV[vi].rearrange("p (h d) -> p h d", h=8), in0=src3,
                            in1=DEC[:, d0:d0 + 8].unsqueeze(2).to_broadcast([128, 8, 128]), op=ALU.mult),
                          r=[R_("fb%d" % c), R_("dec")], w=[R_("qv%d" % vi)])
                    for v in range(4):
                        b = nb()

                        def trq(e, v=v, b=b):
                            inst = None
                            for h in range(8):
                                inst = e.transpose(PB(b)[:, h * 128:(h + 1) * 128], QV[v][:, h * 128:(h + 1) * 128], IDB[:, :])
                            return inst
                        A("pe", trq, r=[R_("qv%d" % v), r_idb], w=[bank_res[b]])
                        en = evac_eng()
                        A(en, copy_op(en, QT[c][v], PB(b).rearrange("p (h t) -> p h t", h=8)),
                          r=[bank_res[b]], w=[R_("qt%d%d" % (c, v))])
                    for d_ in range(2):
                        kw = QV[4 + d_]
                        for hp in range(4):
                            b = nb()

                            def mmu(e, hp=hp, b=b, kw=kw, c=c):
                                inst = None
                                for j in range(2):
                                    h = hp * 2 + j
                                    inst = e.matmul(PS[:, b, j * 256:(j + 1) * 256], kw[:, h * 128:(h + 1) * 128],
                                                    Vb[c][:, h * 256:(h + 1) * 256], start=True, stop=True)
                                return inst
                            A("pe", mmu, r=[R_("qv%d" % (4 + d_)), R_("vb%d" % c)], w=[bank_res[b]])
                            ps3 = PS[:, b, :].rearrange("p (j v) -> p j v", j=2)
                            uf = FB[4 + d_][:, :].rearrange("p (h v) -> p h v", h=8)
                            if c == 0:
                                A("dve", lambda e, uf=uf, hp=hp, ps3=ps3: e.tensor_copy(out=uf[:, hp * 2:hp * 2 + 2, :], in_=ps3),
                                  r=[bank_res[b]], w=[R_("uf%d" % d_)])
                                if d_ == 0:
                                    A("act", lambda e, hp=hp, ps3=ps3: e.activation(out=UB[0][:, hp * 2:hp * 2 + 2, :], in_=ps3,
                                                                                  func=AF.Copy),
                                      r=[bank_res[b]], w=[R_("ub0")])
                            else:
                                if d_ == 1:
                                    A("act", lambda e, hp=hp, ps3=ps3: e.activation(out=UB[1][:, hp * 2:hp * 2 + 2, :], in_=ps3,
                                                                                  func=AF.Copy),
                                      r=[bank_res[b]], w=[R_("ub1")])
                                sfin = ST[:, 512 + d_ * 512: 1024 + d_ * 512].rearrange("p (j v) -> p j v", j=2)
                                for j in range(2):
                                    h = hp * 2 + j
                                    if d_ == 0:
                                        A("dve", lambda e, j=j, h=h, uf=uf, ps3=ps3, sfin=sfin: e.scalar_tensor_tensor(
                                            out=sfin[:, j, :], in0=uf[:, h, :], scalar=CD[:, h:h + 1],
                                            in1=ps3[:, j, :], op0=ALU.mult, op1=ALU.add),
                                          r=[bank_res[b], R_("uf0"), R_("cd")], w=[R_("sfin0")])
                                    else:
                                        A("dve", lambda e, j=j, h=h, uf=uf, ps3=ps3, sfin=sfin: e.scalar_tensor_tensor(
                                            out=sfin[:, j, :], in0=ps3[:, j, :], scalar=CD[:, 8 + h:9 + h],
                                            in1=uf[:, h, :], op0=ALU.mult, op1=ALU.add),
                                          r=[bank_res[b], R_("uf1"), R_("cd")], w=[R_("sfin1")])
                                A("sp", lambda e, d_=d_, hp=hp, sfin=sfin, s=s: e.dma_start(
                                    out=nst[s, l, d_, hp * 2:hp * 2 + 2].rearrange("h k v -> k h v"), in_=sfin),
                                  r=[R_("sfin%d" % d_)], dma=True)
                for c in range(2):
                    t = 2 * s + c
                    for half in range(2):
                        b = nb()

                        def mma(e, half=half, b=b, c=c):
                            inst = None
                            for j in range(4):
                                h = half * 4 + j
                                inst = e.matmul(PS[:, b, j * 128:(j + 1) * 128], QT[c][3][:, h, :], QT[c][0][:, h, :],
                                                start=True, stop=True)
                            return inst
                        A("pe", mma, r=[R_("qt%d3" % c), R_("qt%d0" % c)], w=[bank_res[b]])
                        A("dve", lambda e, half=half, b=b: e.tensor_tensor(
                            out=ATT[:, half * 4:(half + 1) * 4, :], in0=PS[:, b, :].rearrange("p (j t) -> p j t", j=4),
                            in1=MK[:, half * 4:(half + 1) * 4, :], op=ALU.mult),
                          r=[bank_res[b], R_("mask")], w=[R_("att")])
                    for hp in range(4):
                        b = nb()

                        def mmo(e, hp=hp, b=b, c=c):
                            inst = None
                            for j in range(2):
                                h = hp * 2 + j
                                o_ = PS[:, b, j * 256:(j + 1) * 256]
                                e.matmul(o_, ATT[:, h, :], Vb[c][:, h * 256:(h + 1) * 256], start=True, stop=False)
                                if c == 1:
                                    inst = e.matmul(o_, QT[c][1][:, h, :], UB[0][:, h, :], start=False, stop=True)
                                else:
                                    inst = e.matmul(o_, QT[c][2][:, h, :], UB[1][:, h, :], start=False, stop=True)
                            return inst
                        A("pe", mmo, r=[R_("att"), R_("vb%d" % c), R_("qt%d1" % c), R_("qt%d2" % c), R_("ub0"), R_("ub1")],
                          w=[bank_res[b]])
                        for j in range(2):
                            h = hp * 2 + j
                            o_ = PS[:, b, j * 256:(j + 1) * 256]
                            st6 = SM2[:, 80 + h * 8:86 + h * 8]
                            mv = SM2[:, 160 + h * 2:162 + h * 2]
                            rn = "hln%d" % h
                            A("dve", lambda e, o_=o_, st6=st6: e.bn_stats(out=st6, in_=o_), r=[bank_res[b]], w=[R_(rn)])
                            A("dve", lambda e, st6=st6, mv=mv: e.bn_aggr(out=mv, in_=st6), r=[R_(rn)], w=[R_(rn)])
                            A("dve", lambda e, mv=mv: e.tensor_scalar(out=mv[:, 1:2], in0=mv[:, 1:2], scalar1=EPS, scalar2=None,
                                                                      op0=ALU.add), r=[R_(rn)], w=[R_(rn)])
                            A("act", lambda e, mv=mv: e.activation(out=mv[:, 1:2], in_=mv[:, 1:2], func=AF.Sqrt), r=[R_(rn)], w=[R_(rn)])
                            A("dve", lambda e, mv=mv: e.reciprocal(out=mv[:, 1:2], in_=mv[:, 1:2]), r=[R_(rn)], w=[R_(rn)])
                            tmpo = ST[:, 1536 + (h % 2) * 256:1792 + (h % 2) * 256]
                            A("dve", lambda e, o_=o_, mv=mv, tmpo=tmpo: e.tensor_scalar(
                                out=tmpo, in0=o_, scalar1=mv[:, 0:1], scalar2=mv[:, 1:2], op0=ALU.subtract, op1=ALU.mult),
                              r=[bank_res[b], R_(rn)], w=[R_("tmpo%d" % (h % 2))])
                            A("pool", lambda e, h=h, c=c, tmpo=tmpo: e.tensor_tensor(
                                out=RETO[:, h * 256:(h + 1) * 256], in0=tmpo, in1=FB[2 + c][:, h * 256:(h + 1) * 256], op=ALU.mult),
                              r=[R_("tmpo%d" % (h % 2)), R_("fb%d" % (2 + c))], w=[R_("reto")])
                    store_branch(0, RETO, "reto", t)
                for c in range(2):
                    gmlp_chunk(2 * s + c)
                for c in range(2):
                    t = 2 * s + c
                    rows = slice(t * 128, (t + 1) * 128)
                    A("pool", lambda e, c=c, rows=rows: e.dma_start(out=DQb[c], in_=Z_[rows, O_DQ:O_DK]), w=[R_("dq%d" % c)], dma=True)
                    A("pool", lambda e, c=c, rows=rows: e.dma_start(out=DKb[c], in_=Z_[rows, O_DK:O_DV]), w=[R_("dk%d" % c)], dma=True)
                    A("pool", lambda e, c=c, rows=rows: e.dma_start(out=DVb[c], in_=Z_[rows, O_DV:O_GT]), w=[R_("dv%d" % c)], dma=True)
                    for which in range(2):
                        for half in range(2):
                            b = nb()
                            src = DQb[c] if which == 0 else DKb[c]

                            def trd(e, half=half, b=b, src=src):
                                inst = None
                                for j in range(8):
                                    cc = half * 8 + j
                                    inst = e.transpose(PB(b)[:, j * 128:(j + 1) * 128], src[:, cc * 128:(cc + 1) * 128], IDB[:, :])
                                return inst
                            A("pe", trd, r=[R_(("dq%d" if which == 0 else "dk%d") % c), r_idb], w=[bank_res[b]])
                            en = evac_eng()
                            if which == 0:
                                dst = DQT[c][:, half * 8:(half + 1) * 8, :]
                                A(en, copy_op(en, dst, PB(b).rearrange("p (j t) -> p j t", j=8)), r=[bank_res[b]], w=[R_("dqt%d" % c)])
                            else:
                                dst = DKT[:, half * 8:(half + 1) * 8, c * 128:(c + 1) * 128]
                                A(en, copy_op(en, dst, PB(b).rearrange("p (j t) -> p j t", j=8)), r=[bank_res[b]], w=[R_("dkt")])
                for c in range(2):
                    t = 2 * s + c
                    for h in range(8):
                        b = nb()

                        def mms(e, h=h, b=b, c=c):
                            inst = None
                            for j in range(2):
                                inst = e.matmul(PS[:, b, j * 256:(j + 1) * 256], DQT[c][:, 2 * h + j, :], DKT[:, 2 * h + j, :],
                                                start=True, stop=True)
                            return inst
                        A("pe", mms, r=[R_("dqt%d" % c), R_("dkt")], w=[bank_res[b]])
                        ps3 = PS[:, b, :].rearrange("p (j k) -> p j k", j=2)
                        mx = SM2[:, 210:212]
                        sm_ = SM2[:, 212:214]
                        A("dve", lambda e, ps3=ps3: e.tensor_reduce(out=mx, in_=ps3, axis=AX.X, op=ALU.max), r=[bank_res[b]], w=[R_("mx")])
                        A("dve", lambda e: e.tensor_scalar(out=mx, in0=mx, scalar1=-a_sc, scalar2=None, op0=ALU.mult),
                          r=[R_("mx")], w=[R_("mx")])
                        for j in range(2):
                            A("act", lambda e, j=j, ps3=ps3: e.activation(out=EX[:, j, :], in_=ps3[:, j, :], func=AF.Exp, scale=a_sc,
                                                                         bias=mx[:, j:j + 1], accum_out=sm_[:, j:j + 1]),
                              r=[bank_res[b], R_("mx")], w=[R_("ex"), R_("sm%d" % j)])
                        A("dve", lambda e: e.reciprocal(out=sm_, in_=sm_), r=[R_("sm0"), R_("sm1")], w=[R_("sm0"), R_("sm1")])
                        A("dve", lambda e: e.tensor_tensor(out=sm_[:, 1:2], in0=sm_[:, 1:2], in1=NLAM, op=ALU.mult),
                          r=[R_("sm1"), R_("nlam")], w=[R_("sm1")])
                        A("dve", lambda e: e.tensor_scalar(out=EX[:, 1, :], in0=EX[:, 1, :], scalar1=sm_[:, 1:2], scalar2=None,
                                                           op0=ALU.mult), r=[R_("ex"), R_("sm1")], w=[R_("ex")])
                        A("dve", lambda e: e.scalar_tensor_tensor(out=AB, in0=EX[:, 0, :], scalar=sm_[:, 0:1], in1=EX[:, 1, :],
                                                                  op0=ALU.mult, op1=ALU.add),
                          r=[R_("ex"), R_("sm0")], w=[R_("ab")])
                        b2 = nb()

                        def tra(e, b2=b2):
                            inst = None
                            for kb in range(2):
                                inst = e.transpose(PB(b2)[:, kb * 128:(kb + 1) * 128], AB[:, kb * 128:(kb + 1) * 128], IDB[:, :])
                            return inst
                        A("pe", tra, r=[R_("ab"), r_idb], w=[bank_res[b2]])
                        A("act", lambda e, b2=b2: e.activation(out=AT, in_=PB(b2)[:, 0:256].rearrange("p (k t) -> p k t", k=2),
                                                              func=AF.Copy), r=[bank_res[b2]], w=[R_("at")])
                        b3 = nb()

                        def mmpv(e, h=h, b3=b3):
                            inst = None
                            for kb in range(2):
                                inst = e.matmul(PS[:, b3, 0:256], AT[:, kb, :], DVb[kb][:, h * 256:(h + 1) * 256],
                                                start=(kb == 0), stop=(kb == 1))
                            return inst
                        A("pe", mmpv, r=[R_("at"), R_("dv0"), R_("dv1")], w=[bank_res[b3]])
                        ssd = SM2[:, 220:221]
                        A("act", lambda e, b3=b3: e.activation(out=EX[:, 0, :], in_=PS[:, b3, 0:256], func=AF.Square, accum_out=ssd),
                          r=[bank_res[b3]], w=[R_("ex"), R_("ssd")])
                        rstd_from(ssd, 256.0, "ssd")
                        A("dve", lambda e, h=h, b3=b3: e.scalar_tensor_tensor(
                            out=DIFO[:, h * 256:(h + 1) * 256], in0=PS[:, b3, 0:256], scalar=ssd, in1=SUBB[:, :],
                            op0=ALU.mult, op1=ALU.mult), r=[bank_res[b3], R_("ssd"), R_("subb")], w=[R_("difo")])
                    store_branch(2, DIFO, "difo", t)

            if mode != "S":
                return
            NCH = SLEN // 128
            NK = PAST + SLEN
            NKB = NK // 128
            NKP = (NK + 511) // 512
            sch.barrier()
            ROPE = ST[:, 0:256]

            def load_rope(i):
                A("sp", lambda e: e.dma_start(out=ST[:, 0:128], in_=ropec[i * 128:(i + 1) * 128, :]), w=[R_("rope")], dma=True)
                A("sp", lambda e: e.dma_start(out=ST[:, 128:256], in_=ropes[i * 128:(i + 1) * 128, :]), w=[R_("rope")], dma=True)

            def rope(dst, src, nh, tA, tB, nsrc, ndst, nA, nB):
                w_ = nh * 128
                src3 = src.rearrange("p (h d) -> p h d", h=nh)
                tA3 = tA[:, 0:w_].rearrange("p (h d) -> p h d", h=nh)
                tB3 = tB[:, 0:w_].rearrange("p (h d) -> p h d", h=nh)
                s4 = src.rearrange("p (m b j) -> p m b j", b=2, j=32)
                t4 = tB[:, 0:w_].rearrange("p (m b j) -> p m b j", b=2, j=32)
                Cb = ST[:, 0:128].unsqueeze(1).to_broadcast([128, nh, 128])
                Sb = ST[:, 128:256].unsqueeze(1).to_broadcast([128, nh, 128])
                A("dve", lambda e: e.tensor_tensor(out=tA3, in0=src3, in1=Cb, op=ALU.mult), r=[R_(nsrc), R_("rope")], w=[R_(nA)])
                A("pool", lambda e: e.tensor_copy(out=t4[:, :, 0, :], in_=s4[:, :, 1, :]), r=[R_(nsrc)], w=[R_(nB)])
                A("act", lambda e: e.activation(out=t4[:, :, 1, :], in_=s4[:, :, 0, :], func=AF.Copy), r=[R_(nsrc)], w=[R_(nB)])
                A("dve", lambda e: e.tensor_tensor(out=tB3, in0=tB3, in1=Sb, op=ALU.mult), r=[R_(nB), R_("rope")], w=[R_(nB)])
                A("pool", lambda e: e.tensor_tensor(out=dst, in0=tA[:, 0:w_], in1=tB[:, 0:w_], op=ALU.add),
                  r=[R_(nA), R_(nB)], w=[R_(ndst)])

            TKS = BIGW[:, 27136:29184].rearrange("p (j t) -> p j t", j=16)

            def transpose16_to(src, nsrc, dram_dst):
                for half in range(2):
                    b = nb()

                    def trk(e, half=half, b=b):
                        inst = None
                        for j in range(8):
                            cc = half * 8 + j
                            inst = e.transpose(PB(b)[:, j * 128:(j + 1) * 128], src[:, cc * 128:(cc + 1) * 128], IDB[:, :])
                        return inst
                    A("pe", trk, r=[R_(nsrc), r_idb], w=[bank_res[b]])
                    en = evac_eng()
                    A(en, copy_op(en, TKS[:, half * 8:(half + 1) * 8, :], PB(b).rearrange("p (j t) -> p j t", j=8)),
                      r=[bank_res[b]], w=[R_("bos")])
                A("sp", lambda e: e.dma_start(out=dram_dst, in_=TKS), r=[R_("bos")], dma=True)

            KB0 = DKb[0]
            for kb in range(PAST // 128):
                A("pool", lambda e, kb=kb: e.dma_start(out=KB0, in_=ck[l, kb * 128:(kb + 1) * 128, :]), w=[R_("dk0")], dma=True)
                transpose16_to(KB0, "dk0", kTD[:, :, kb * 128:(kb + 1) * 128].rearrange("j d t -> d j t"))
            for i in range(NCH):
                rows = slice(i * 128, (i + 1) * 128)
                load_rope(i)
                A("sp", lambda e, rows=rows: e.dma_start(out=FB[0][:, :], in_=Z_[rows, O_DK:O_DV]), w=[R_("fb0")], dma=True)
                rope(KB0, FB[0][:, :], 16, FB[1], FB[2], "fb0", "dk0", "fb1", "fb2")
                transpose16_to(KB0, "dk0", kTD[:, :, PAST + i * 128:PAST + (i + 1) * 128].rearrange("j d t -> d j t"))
                A("sp", lambda e, rows=rows: e.dma_start(out=FB[3][:, :], in_=Z_[rows, O_DQ:O_DK]), w=[R_("fb3")], dma=True)
                rope(DQb[0], FB[3][:, :], 16, FB[4], FB[5], "fb3", "dq0", "fb4", "fb5")
                transpose16_to(DQb[0], "dq0", qTD[:, :, i * 128:(i + 1) * 128].rearrange("j d t -> d j t"))
            sch.barrier()

            SCUR = FB[5][:, :].rearrange("p (h v) -> p h v", h=8)
            WBb = DEC[:, 24:32].unsqueeze(2).to_broadcast([128, 8, 128])
            A("sp", lambda e: e.dma_start(out=SCUR, in_=sr[l, 1].rearrange("h k v -> k h v")), w=[R_("scur")], dma=True)
            for i in range(NCH - 1, -1, -1):
                rows = slice(i * 128, (i + 1) * 128)
                A("act", lambda e: e.activation(out=UB[1], in_=SCUR, func=AF.Copy), r=[R_("scur")], w=[R_("ub1")])
                A("sp", lambda e, i=i: e.dma_start(out=sbD[i].rearrange("k (h v) -> k h v", h=8), in_=UB[1]), r=[R_("ub1")], dma=True)
                load_rope(i)
                A("sp", lambda e, rows=rows: e.dma_start(out=FB[0][:, 0:1024], in_=Z_[rows, O_RK:O_RV]), w=[R_("fb0")], dma=True)
                A("pool", lambda e, rows=rows: e.dma_start(out=Vb[0], in_=Z_[rows, O_RV:O_RG]), w=[R_("vb0")], dma=True)
                rope(FB[0][:, 1024:2048], FB[0][:, 0:1024], 8, FB[1], FB[2], "fb0", "fb0r", "fb1", "fb2")
                A("dve", lambda e: e.tensor_tensor(out=QV[5].rearrange("p (h d) -> p h d", h=8),
                                                   in0=FB[0][:, 1024:2048].rearrange("p (h d) -> p h d", h=8), in1=WBb, op=ALU.mult),
                  r=[R_("fb0r"), R_("dec")], w=[R_("qv5")])
                for hp in range(4):
                    b = nb()

                    def mmub(e, hp=hp, b=b):
                        inst = None
                        for j in range(2):
                            h = hp * 2 + j
                            inst = e.matmul(PS[:, b, j * 256:(j + 1) * 256], QV[5][:, h * 128:(h + 1) * 128],
                                            Vb[0][:, h * 256:(h + 1) * 256], start=True, stop=True)
                        return inst
                    A("pe", mmub, r=[R_("qv5"), R_("vb0")], w=[bank_res[b]])
                    for j in range(2):
                        h = hp * 2 + j
                        A("dve", lambda e, h=h, j=j, b=b: e.scalar_tensor_tensor(
                            out=SCUR[:, h, :], in0=SCUR[:, h, :], scalar=CD[:, 8 + h:9 + h], in1=PS[:, b, j * 256:(j + 1) * 256],
                            op0=ALU.mult, op1=ALU.add), r=[bank_res[b], R_("cd"), R_("ub1")], w=[R_("scur")])
            sch.barrier()

            A("sp", lambda e: e.dma_start(out=SCUR, in_=sr[l, 0].rearrange("h k v -> k h v")), w=[R_("scur")], dma=True)
            for i in range(NCH):
                rows = slice(i * 128, (i + 1) * 128)
                A("act", lambda e: e.activation(out=UB[0], in_=SCUR, func=AF.Copy), r=[R_("scur")], w=[R_("ub0")])
                A("sp", lambda e, i=i: e.dma_start(out=UB[1], in_=sbD[i].rearrange("k (h v) -> k h v", h=8)), w=[R_("ub1")], dma=True)
                load_rope(i)
                A("sp", lambda e, rows=rows: e.dma_start(out=FB[0][:, :], in_=Z_[rows, O_RQ:O_RV]), w=[R_("fb0")], dma=True)
                A("sp", lambda e, rows=rows: e.dma_start(out=FB[4][:, :], in_=Z_[rows, O_RG:O_MU]), w=[R_("fb4")], dma=True)
                A("pool", lambda e, rows=rows: e.dma_start(out=Vb[0], in_=Z_[rows, O_RV:O_RG]), w=[R_("vb0")], dma=True)
                A("act", lambda e: e.activation(out=FB[4][:, :], in_=FB[4][:, :], func=AF.Silu), r=[R_("fb4")], w=[R_("fb4")])
                rope(FB[1][:, :], FB[0][:, :], 16, FB[2], FB[3], "fb0", "fb1", "fb2", "fb3")
                q3 = FB[1][:, 0:1024].rearrange("p (h d) -> p h d", h=8)
                k3 = FB[1][:, 1024:2048].rearrange("p (h d) -> p h d", h=8)
                A("dve", lambda e: e.tensor_copy(out=QV[0], in_=FB[1][:, 0:1024]), r=[R_("fb1")], w=[R_("qv0")])
                A("pool", lambda e: e.tensor_copy(out=QV[3], in_=FB[1][:, 1024:2048]), r=[R_("fb1")], w=[R_("qv3")])
                for v, (src3, d0, vi) in enumerate([(q3, 0, 1), (q3, 8, 2), (k3, 16, 4)]):
                    en = "dve" if v % 2 == 0 else "pool"
                    A(en, lambda e, vi=vi, src3=src3, d0=d0: e.tensor_tensor(
                        out=QV[vi].rearrange("p (h d) -> p h d", h=8), in0=src3,
                        in1=DEC[:, d0:d0 + 8].unsqueeze(2).to_broadcast([128, 8, 128]), op=ALU.mult),
                      r=[R_("fb1"), R_("dec")], w=[R_("qv%d" % vi)])
                for v in range(4):
                    b = nb()

                    def trq(e, v=v, b=b):
                        inst = None
                        for h in range(8):
                            inst = e.transpose(PB(b)[:, h * 128:(h + 1) * 128], QV[v][:, h * 128:(h + 1) * 128], IDB[:, :])
                        return inst
                    A("pe", trq, r=[R_("qv%d" % v), r_idb], w=[bank_res[b]])
                    en = evac_eng()
                    A(en, copy_op(en, QT[0][v], PB(b).rearrange("p (h t) -> p h t", h=8)), r=[bank_res[b]], w=[R_("qt0%d" % v)])
                for half in range(2):
                    b = nb()

                    def mma(e, half=half, b=b):
                        inst = None
                        for j in range(4):
                            h = half * 4 + j
                            inst = e.matmul(PS[:, b, j * 128:(j + 1) * 128], QT[0][3][:, h, :], QT[0][0][:, h, :], start=True, stop=True)
                        return inst
                    A("pe", mma, r=[R_("qt03"), R_("qt00")], w=[bank_res[b]])
                    A("dve", lambda e, half=half, b=b: e.tensor_tensor(
                        out=ATT[:, half * 4:(half + 1) * 4, :], in0=PS[:, b, :].rearrange("p (j t) -> p j t", j=4),
                        in1=MK[:, half * 4:(half + 1) * 4, :], op=ALU.mult), r=[bank_res[b], R_("mask")], w=[R_("att")])
                for hp in range(4):
                    b = nb()

                    def mmo(e, hp=hp, b=b):
                        inst = None
                        for j in range(2):
                            h = hp * 2 + j
                            o_ = PS[:, b, j * 256:(j + 1) * 256]
                            e.matmul(o_, ATT[:, h, :], Vb[0][:, h * 256:(h + 1) * 256], start=True, stop=False)
                            e.matmul(o_, QT[0][1][:, h, :], UB[0][:, h, :], start=False, stop=False)
                            inst = e.matmul(o_, QT[0][2][:, h, :], UB[1][:, h, :], start=False, stop=True)
                        return inst
                    A("pe", mmo, r=[R_("att"), R_("vb0"), R_("qt01"), R_("qt02"), R_("ub0"), R_("ub1")], w=[bank_res[b]])
                    for j in range(2):
                        h = hp * 2 + j
                        o_ = PS[:, b, j * 256:(j + 1) * 256]
                        st6 = SM2[:, 80 + h * 8:86 + h * 8]
                        mv = SM2[:, 160 + h * 2:162 + h * 2]
                        rn = "hln%d" % h
                        A("dve", lambda e, o_=o_, st6=st6: e.bn_stats(out=st6, in_=o_), r=[bank_res[b]], w=[R_(rn)])
                        A("dve", lambda e, st6=st6, mv=mv: e.bn_aggr(out=mv, in_=st6), r=[R_(rn)], w=[R_(rn)])
                        A("dve", lambda e, mv=mv: e.tensor_scalar(out=mv[:, 1:2], in0=mv[:, 1:2], scalar1=EPS, scalar2=None,
                                                                  op0=ALU.add), r=[R_(rn)], w=[R_(rn)])
                        A("act", lambda e, mv=mv: e.activation(out=mv[:, 1:2], in_=mv[:, 1:2], func=AF.Sqrt), r=[R_(rn)], w=[R_(rn)])
                        A("dve", lambda e, mv=mv: e.reciprocal(out=mv[:, 1:2], in_=mv[:, 1:2]), r=[R_(rn)], w=[R_(rn)])
                        tmpo = ST[:, 1536 + (h % 2) * 256:1792 + (h % 2) * 256]
                        A("dve", lambda e, o_=o_, mv=mv, tmpo=tmpo: e.tensor_scalar(
                            out=tmpo, in0=o_, scalar1=mv[:, 0:1], scalar2=mv[:, 1:2], op0=ALU.subtract, op1=ALU.mult),
                          r=[bank_res[b], R_(rn)], w=[R_("tmpo%d" % (h % 2))])
                        A("pool", lambda e, h=h, tmpo=tmpo: e.tensor_tensor(
                            out=RETO[:, h * 256:(h + 1) * 256], in0=tmpo, in1=FB[4][:, h * 256:(h + 1) * 256], op=ALU.mult),
                          r=[R_("tmpo%d" % (h % 2)), R_("fb4")], w=[R_("reto")])
                store_branch(0, RETO, "reto", i)
                for hp in range(4):
                    b = nb()

                    def mmuf(e, hp=hp, b=b):
                        inst = None
                        for j in range(2):
                            h = hp * 2 + j
                            inst = e.matmul(PS[:, b, j * 256:(j + 1) * 256], QV[4][:, h * 128:(h + 1) * 128],
                                            Vb[0][:, h * 256:(h + 1) * 256], start=True, stop=True)
                        return inst
                    A("pe", mmuf, r=[R_("qv4"), R_("vb0")], w=[bank_res[b]])
                    for j in range(2):
                        h = hp * 2 + j
                        A("dve", lambda e, h=h, j=j, b=b: e.scalar_tensor_tensor(
                            out=SCUR[:, h, :], in0=SCUR[:, h, :], scalar=CD[:, h:h + 1], in1=PS[:, b, j * 256:(j + 1) * 256],
                            op0=ALU.mult, op1=ALU.add), r=[bank_res[b], R_("cd"), R_("ub0")], w=[R_("scur")])
            sch.barrier()

            for i in range(NCH):
                gmlp_chunk(i)
            sch.barrier()

            KTH = BIGW[:, 0:2 * NK].rearrange("p (j t) -> p j t", j=2)
            VH = BIGW[:, 9216:9216 + NKB * 256].rearrange("p (k v) -> p k v", k=NKB)
            QTH = BIGW[:, 18432:18432 + 2 * SLEN].rearrange("p (j t) -> p j t", j=2)
            E32 = BIGA[:, 0:4 * NK].bitcast(F32).rearrange("p (j t) -> p j t", j=2)
            ABS = BIGA[:, 18432:18432 + NK]
            ATS = BIGA[:, 23040:23040 + NK].rearrange("p (k t) -> p k t", k=NKB)
            DFH = BIGA[:, 27648:27904]
            DFT = BIGA[:, 27904:28160].rearrange("p (j t) -> p j t", j=2)
            MXP = SM2[:, 224:224 + 2 * NKP].rearrange("p (j k) -> p j k", j=2)
            SMP = SM2[:, 256:256 + 2 * NKP].rearrange("p (j k) -> p j k", j=2)
            mx = SM2[:, 210:212]
            sm_ = SM2[:, 212:214]
            ssd = SM2[:, 220:221]
            for h in range(8):
                A("sp", lambda e, h=h: e.dma_start(out=KTH, in_=kTD[2 * h:2 * h + 2, :, :].rearrange("j d t -> d j t")),
                  w=[R_("kth")], dma=True)
                A("sp", lambda e, h=h: e.dma_start(out=QTH, in_=qTD[2 * h:2 * h + 2, :, :].rearrange("j d t -> d j t")),
                  w=[R_("qth")], dma=True)
                A("pool", lambda e, h=h: e.dma_start(out=VH[:, 0:PAST // 128, :],
                                                     in_=cv[l, :, h * 256:(h + 1) * 256].rearrange("(k p) v -> p k v", p=128)),
                  w=[R_("vh")], dma=True)
                for k0 in range(0, NCH, 8):
                    k1 = min(NCH, k0 + 8)
                    A("pool", lambda e, h=h, k0=k0, k1=k1: e.dma_start(
                        out=VH[:, PAST // 128 + k0:PAST // 128 + k1, :],
                        in_=Z_[k0 * 128:k1 * 128, O_DV + h * 256:O_DV + (h + 1) * 256].rearrange("(k p) v -> p k v", p=128)),
                      w=[R_("vh")], dma=True)
                for i in range(NCH):
                    for ps_ in range(2):
                        for j in range(2):
                            for kp in range(NKP):
                                c0 = kp * 512
                                c1 = min(NK, c0 + 512)
                                b = nb()
                                A("pe", lambda e, b=b, j=j, c0=c0, c1=c1, i=i: e.matmul(
                                    PS[:, b, 0:c1 - c0], QTH[:, j, i * 128:(i + 1) * 128], KTH[:, j, c0:c1], start=True, stop=True),
                                  r=[R_("qth"), R_("kth")], w=[bank_res[b]])
                                if ps_ == 0:
                                    A("dve", lambda e, b=b, j=j, kp=kp, c0=c0, c1=c1: e.tensor_reduce(
                                        out=MXP[:, j, kp:kp + 1], in_=PS[:, b, 0:c1 - c0], axis=AX.X, op=ALU.max),
                                      r=[bank_res[b]], w=[R_("mxp")])
                                else:
                                    A("act", lambda e, b=b, j=j, kp=kp, c0=c0, c1=c1: e.activation(
                                        out=E32[:, j, c0:c1], in_=PS[:, b, 0:c1 - c0], func=AF.Exp, scale=a_sc, bias=mx[:, j:j + 1],
                                        accum_out=SMP[:, j, kp:kp + 1]), r=[bank_res[b], R_("mx")], w=[R_("e32"), R_("smp")])
                        if ps_ == 0:
                            A("dve", lambda e: e.tensor_reduce(out=mx, in_=MXP, axis=AX.X, op=ALU.max), r=[R_("mxp")], w=[R_("mx")])
                            A("dve", lambda e: e.tensor_scalar(out=mx, in0=mx, scalar1=-a_sc, scalar2=None, op0=ALU.mult),
                              r=[R_("mx")], w=[R_("mx")])
                    A("dve", lambda e: e.tensor_reduce(out=sm_, in_=SMP, axis=AX.X, op=ALU.add), r=[R_("smp")], w=[R_("sm")])
                    A("dve", lambda e: e.reciprocal(out=sm_, in_=sm_), r=[R_("sm")], w=[R_("sm")])
                    A("dve", lambda e: e.tensor_tensor(out=sm_[:, 1:2], in0=sm_[:, 1:2], in1=NLAM, op=ALU.mult),
                      r=[R_("sm"), R_("nlam")], w=[R_("sm")])
                    A("pool", lambda e: e.tensor_scalar(out=E32[:, 1, :], in0=E32[:, 1, :], scalar1=sm_[:, 1:2], scalar2=None,
                                                        op0=ALU.mult), r=[R_("e32"), R_("sm")], w=[R_("e32")])
                    A("dve", lambda e: e.scalar_tensor_tensor(out=ABS, in0=E32[:, 0, :], scalar=sm_[:, 0:1], in1=E32[:, 1, :],
                                                              op0=ALU.mult, op1=ALU.add), r=[R_("e32"), R_("sm")], w=[R_("abs")])
                    for g0 in range(0, NKB, 8):
                        g1 = min(NKB, g0 + 8)
                        b = nb()

                        def tra(e, g0=g0, g1=g1, b=b):
                            inst = None
                            for kb in range(g0, g1):
                                inst = e.transpose(PB(b)[:, (kb - g0) * 128:(kb - g0 + 1) * 128], ABS[:, kb * 128:(kb + 1) * 128], IDB[:, :])
                            return inst
                        A("pe", tra, r=[R_("abs"), r_idb], w=[bank_res[b]])
                        en = evac_eng()
                        A(en, copy_op(en, ATS[:, g0:g1, :], PB(b)[:, 0:(g1 - g0) * 128].rearrange("p (k t) -> p k t", k=g1 - g0)),
                          r=[bank_res[b]], w=[R_("ats")])
                    b3 = nb()

                    def mmpv(e, b3=b3):
                        inst = None
                        for kb in range(NKB):
                            inst = e.matmul(PS[:, b3, 0:256], ATS[:, kb, :], VH[:, kb, :], start=(kb == 0), stop=(kb == NKB - 1))
                        return inst
                    A("pe", mmpv, r=[R_("ats"), R_("vh")], w=[bank_res[b3]])
                    A("act", lambda e, b3=b3: e.activation(out=ST[:, 512:768], in_=PS[:, b3, 0:256], func=AF.Square, accum_out=ssd),
                      r=[bank_res[b3]], w=[R_("sq"), R_("ssd")])
                    rstd_from(ssd, 256.0, "ssd")
                    A("dve", lambda e, b3=b3: e.scalar_tensor_tensor(out=DFH, in0=PS[:, b3, 0:256], scalar=ssd, in1=SUBB[:, :],
                                                                     op0=ALU.mult, op1=ALU.mult),
                      r=[bank_res[b3], R_("ssd"), R_("subb")], w=[R_("dfh")])
                    b4 = nb()

                    def trd2(e, b4=b4):
                        inst = None
                        for j in range(2):
                            inst = e.transpose(PB(b4)[:, j * 128:(j + 1) * 128], DFH[:, j * 128:(j + 1) * 128], IDB[:, :])
                        return inst
                    A("pe", trd2, r=[R_("dfh"), r_idb], w=[bank_res[b4]])
                    A("act", lambda e, b4=b4: e.activation(out=DFT, in_=PB(b4)[:, 0:256].rearrange("p (j t) -> p j t", j=2), func=AF.Copy),
                      r=[bank_res[b4]], w=[R_("dft")])
                    A("sp", lambda e, h=h, i=i: e.dma_start(
                        out=BO_[2, h * 256:(h + 1) * 256, i * 128:(i + 1) * 128].rearrange("(j p) t -> p j t", p=128), in_=DFT),
                      r=[R_("dft")], dma=True)


        def g2_phase(gi, l):
            Z_, BO_, XW_, NT_ = X['Z'], X['BO'], X['XW'], X['NT']
            for th in range((NT_ + 3) // 4):
                t0 = th * 4
                ntl = min(4, NT_ - t0)
                bo = BIGA[:, 0:3 * 16 * 512].rearrange("p (b k t) -> p b k t", b=3, k=16)
                r_bo = [Res() for _ in range(3)]
                for bi in range(3):
                    A("sp", lambda e, bi=bi, t0=t0, ntl=ntl: e.dma_start(
                        out=bo[:, bi, :, 0:ntl * 128],
                        in_=BO_[bi, :, t0 * 128:(t0 + ntl) * 128].rearrange("(k p) t -> p k t", p=128)),
                      w=[r_bo[bi]], dma=True)

                def wsrc(n3, k0, nk):
                    n, bi = n3 // 3, n3 % 3
                    return [(0, 512, w_br[l, bi, k0 * 128:(k0 + nk) * 128, n * 512:(n + 1) * 512]
                             .rearrange("(k p) c -> p k c", p=128))]
                acc = [FT[:, i * 512:(i + 1) * 512] for i in range(4)]
                acc_res = [Res() for _ in range(4)]
                gts = [FT[:, 2048 + i * 512:2048 + (i + 1) * 512] for i in range(4)]
                gts_res = [Res() for _ in range(4)]
                tmps = [FT[:, 4096 + i * 512:4096 + (i + 1) * 512] for i in range(4)]
                tmps_res = [Res() for _ in range(4)]
                cnt = {"g": 0}

                def evac(tt, n3, b):
                    n, bi = n3 // 3, n3 % 3
                    t = t0 + tt
                    i = cnt["g"] % 4
                    cnt["g"] += 1
                    gt = gts[i]
                    c0 = O_GT + bi * D + n * 512
                    A("sp", lambda e: e.dma_start(out=gt, in_=Z_[t * 128:(t + 1) * 128, c0:c0 + 512]), w=[gts_res[i]], dma=True)
                    A("act", lambda e: e.activation(out=gt, in_=gt, func=AF.Sigmoid), r=[gts_res[i]], w=[gts_res[i]])
                    if bi == 0:
                        A("dve", lambda e: e.tensor_tensor(out=acc[tt], in0=gt, in1=psb(b), op=ALU.mult),
                          r=[gts_res[i], bank_res[b]], w=[acc_res[tt]])
                    else:
                        A("dve", lambda e: e.tensor_tensor(out=tmps[i], in0=gt, in1=psb(b), op=ALU.mult),
                          r=[gts_res[i], bank_res[b]], w=[tmps_res[i]])
                        A("pool", lambda e: e.tensor_tensor(out=acc[tt], in0=acc[tt], in1=tmps[i], op=ALU.add),
                          r=[tmps_res[i]], w=[acc_res[tt]])
                    if bi == 2:
                        A("sp", lambda e: e.dma_start(out=mgD[t * 128:(t + 1) * 128, n * 512:(n + 1) * 512], in_=acc[tt]),
                          r=[acc_res[tt]], dma=True)

                gemm(16, lambda kc, tt, n3: bo[:, n3 % 3, kc, tt * 128:(tt + 1) * 128], wsrc, 24, ntl, 128, evac,
                     lambda tt, n3: r_bo[n3 % 3])
                sch.barrier()

        def plain_tile(t, xt, r_x):
            for q in range(8):
                b = 4 + (state["gb"] % 4)
                state["gb"] += 1

                def tr(e, q=q, b=b):
                    inst = None
                    for j in range(4):
                        c = q * 4 + j
                        inst = e.transpose(PS[:, b, j * 128:(j + 1) * 128], xt[:, c * 128:(c + 1) * 128], ident)
                    return inst
                A("pe", tr, r=[r_x, r_ct], w=[bank_res[b]])
                for j in range(4):
                    c = q * 4 + j
                    en = "act" if j % 2 == 0 else "dve"
                    A(en, copy_op(en, actT[:, c, t * 128:(t + 1) * 128], PS[:, b, j * 128:(j + 1) * 128]),
                      r=[bank_res[b]], w=[actT_res[t]])

        SSQ = sb("SSQ", [128, 64], F32)
        r_ssq = [Res() for _ in range(8)]
        r_sqj = [Res(), Res()]

        def raw_evac(t, n, b):
            i = next_st()
            st = ST[:, i * 512:(i + 1) * 512]
            A("dve", lambda e: e.tensor_copy(out=st, in_=psb(b)), r=[bank_res[b]], w=[st_res[i]])
            A("act", lambda e: e.activation(out=FT[:, 8192 + (t % 2) * 512:8704 + (t % 2) * 512], in_=st, func=AF.Square,
                                            accum_out=SSQ[:, t * 8 + n:t * 8 + n + 1]),
              r=[st_res[i]], w=[r_ssq[t], r_sqj[t % 2]])
            A("sp", lambda e: e.dma_start(out=m2D[t * 128:(t + 1) * 128, n * 512:(n + 1) * 512], in_=st),
              r=[st_res[i]], dma=True)

        def resid_epilogue(gi, l, gate_off, ng_idx, xsrc, xdst, then_adaln):
            Z_, BO_, XW_, NT_ = X['Z'], X['BO'], X['XW'], X['NT']
            GB = FT[:, 8192:12288]
            NGB = BIGW[:, 8192:16384].bitcast(F32)
            r_gb, r_ngb = Res(), Res()
            A("sp", lambda e: e.dma_start(out=GB, in_=modD[l, gi, gate_off:gate_off + D].partition_broadcast(128)),
              w=[r_gb], dma=True)
            A("sp", lambda e: e.dma_start(out=NGB, in_=norm_g[l, ng_idx].partition_broadcast(128)), w=[r_ngb], dma=True)
            A("dve", lambda e: e.tensor_tensor(out=GB, in0=GB, in1=NGB, op=ALU.mult), r=[r_ngb], w=[r_gb])
            r_m, r_x = Res(), Res()
            for t in range(NT_):
                mt = FT[:, 0:4096]
                xt = FT[:, 4096:8192]
                rows = slice(t * 128, (t + 1) * 128)
                A("sp", lambda e, rows=rows: e.dma_start(out=mt, in_=m2D[rows, :]), w=[r_m], dma=True)
                A("sp", lambda e, rows=rows: e.dma_start(out=xt, in_=xsrc[rows, :]), w=[r_x], dma=True)
                ss = SM[:, 320 + t:321 + t]
                r_ss = Res()
                A("dve", lambda e, t=t, ss=ss: e.tensor_reduce(out=ss, in_=SSQ[:, t * 8:(t + 1) * 8], axis=AX.X, op=ALU.add),
                  r=[r_ssq[t]], w=[r_ss])
                A("dve", lambda e, ss=ss: e.tensor_scalar(out=ss, in0=ss, scalar1=1.0 / D, scalar2=EPS, op0=ALU.mult, op1=ALU.add),
                  r=[r_ss], w=[r_ss])
                A("act", lambda e, ss=ss: e.activation(out=ss, in_=ss, func=AF.Sqrt), r=[r_ss], w=[r_ss])
                A("dve", lambda e, ss=ss: e.reciprocal(out=ss, in_=ss), r=[r_ss], w=[r_ss])
                A("dve", lambda e, ss=ss: e.scalar_tensor_tensor(out=mt, in0=mt, scalar=ss, in1=GB, op0=ALU.mult, op1=ALU.mult),
                  r=[r_ss, r_gb], w=[r_m])
                A("pool", lambda e: e.tensor_tensor(out=xt, in0=xt, in1=mt, op=ALU.add), r=[r_m], w=[r_x])
                A("sp", lambda e, rows=rows: e.dma_start(out=xdst[rows, :], in_=xt), r=[r_x], dma=True)
                if then_adaln:
                    adaln_tile(gi, 1, t, xt, r_x)

        def g3_phase(gi, l, xsrc):
            Z_, BO_, XW_, NT_ = X['Z'], X['BO'], X['XW'], X['NT']
            r_m = [Res(), Res()]
            for t in range(NT_):
                mt = FT[:, (t % 2) * 4096:(t % 2 + 1) * 4096]
                A("sp", lambda e, mt=mt, t=t: e.dma_start(out=mt, in_=mgD[t * 128:(t + 1) * 128, :]), w=[r_m[t % 2]], dma=True)
                plain_tile(t, mt, r_m[t % 2])
            sch.barrier()

            def wsrc(n, k0, nk):
                return [(0, 512, w_o[l, k0 * 128:(k0 + nk) * 128, n * 512:(n + 1) * 512].rearrange("(k p) c -> p k c", p=128))]
            if cfg.stop == "g3a":
                return
            gemm(32, lambda kc, t, n: actT[:, kc, t * 128:(t + 1) * 128], wsrc, 8, NT_, 128, raw_evac, lambda t, n: actT_res[t])
            sch.barrier()
            if cfg.stop == "g3b":
                return
            resid_epilogue(gi, l, 2 * D, 1, xsrc, XW_, True)
            sch.barrier()

        def g4_phase(gi, l):
            Z_, BO_, XW_, NT_ = X['Z'], X['BO'], X['XW'], X['NT']
            def wsrc(n, k0, nk):
                return [(0, 256, w_up[l, k0 * 128:(k0 + nk) * 128, n * 256:(n + 1) * 256].rearrange("(k p) c -> p k c", p=128)),
                        (256, 512, w_up[l, k0 * 128:(k0 + nk) * 128, FH + n * 256:FH + (n + 1) * 256]
                         .rearrange("(k p) c -> p k c", p=128))]
            fa = [FT[:, i * 256:(i + 1) * 256] for i in range(4)]
            fa_res = [Res() for _ in range(4)]
            fb = [BIGW[:, 16384 + i * 256:16384 + (i + 1) * 256] for i in range(4)]
            fb_res = [Res() for _ in range(4)]
            fs = [FT[:, 2048 + i * 128:2048 + (i + 1) * 128].bitcast(BF16).rearrange("p (j t) -> p j t", j=2) for i in range(4)]
            fs_res = [Res() for _ in range(4)]
            cnt = {"i": 0}

            def evac(t, n, b):
                i = cnt["i"] % 4
                cnt["i"] += 1
                A("act", lambda e: e.activation(out=fa[i], in_=PS[:, b, 0:256], func=AF.Silu), r=[bank_res[b]], w=[fa_res[i]])
                fbt = FT[:, 4096 + i * 128:4096 + (i + 1) * 128].bitcast(BF16)
                A("dve", lambda e: e.tensor_tensor(out=fbt, in0=fa[i], in1=PS[:, b, 256:512], op=ALU.mult),
                  r=[fa_res[i], bank_res[b]], w=[fb_res[i]])
                b2 = 4 + (state["gb"] % 4)
                state["gb"] += 1

                def tr(e):
                    inst = None
                    for j in range(2):
                        inst = e.transpose(PB(b2)[:, j * 128:(j + 1) * 128], fbt[:, j * 128:(j + 1) * 128], IDB[:, :])
                    return inst
                A("pe", tr, r=[fb_res[i], r_idb], w=[bank_res[b2]])
                A("pool" if False else "dve", lambda e: e.tensor_copy(out=fs[i], in_=PB(b2)[:, 0:256].rearrange("p (j t) -> p j t", j=2)),
                  r=[bank_res[b2]], w=[fs_res[i]])
                A("sp", lambda e: e.dma_start(out=fT[n * 256:(n + 1) * 256, t * 128:(t + 1) * 128].rearrange("(j p) t -> p j t", p=128),
                                              in_=fs[i]), r=[fs_res[i]], dma=True)

            gemm(32, lambda kc, t, n: actT[:, kc, t * 128:(t + 1) * 128], wsrc, FH // 256, NT_, 128, evac, lambda t, n: actT_res[t])
            sch.barrier()

        def g5_phase(gi, l, xdst):
            Z_, BO_, XW_, NT_ = X['Z'], X['BO'], X['XW'], X['NT']
            nsub = (FH // 128 + 15) // 16
            ah = [BIGA[:, i * 16384:(i + 1) * 16384].rearrange("p (k t) -> p k t", k=16) for i in range(2)]
            ah_res = [Res(), Res()]
            ctr = 0
            for n in range(8):
                for s_ in range(nsub):
                    k0 = s_ * 16
                    nk = min(16, FH // 128 - k0)
                    sl = ctr % 4
                    ai = ctr % 2
                    ctr += 1
                    A("pool", lambda e, sl=sl, nk=nk, k0=k0, n=n: e.dma_start(
                        out=wslot[sl][:, 0:nk, :],
                        in_=w_dn[l, k0 * 128:(k0 + nk) * 128, n * 512:(n + 1) * 512].rearrange("(k p) c -> p k c", p=128)),
                      w=[wslot_res[sl]], dma=True)
                    A("sp", lambda e, ai=ai, nk=nk, k0=k0: e.dma_start(
                        out=ah[ai][:, 0:nk, 0:NT_ * 128], in_=fT[k0 * 128:(k0 + nk) * 128, 0:NT_ * 128].rearrange("(k p) t -> p k t", p=128)),
                      w=[ah_res[ai]], dma=True)
                    for t in range(NT_):
                        def mm(e, t=t, sl=sl, ai=ai, nk=nk, s_=s_):
                            inst = None
                            for kc in range(nk):
                                inst = e.matmul(PS[:, t, :], ah[ai][:, kc, t * 128:(t + 1) * 128], wslot[sl][:, kc, :],
                                                start=(s_ == 0 and kc == 0), stop=(s_ == nsub - 1 and kc == nk - 1))
                            return inst
                        A("pe", mm, r=[ah_res[ai], wslot_res[sl]], w=[bank_res[t]])
                        if s_ == nsub - 1:
                            raw_evac(t, n, t)
            sch.barrier()
            resid_epilogue(gi, l, 5 * D, 3, XW_, xdst, False)
            sch.barrier()

        for l in range(cfg.layers):
            last = (l == cfg.layers - 1)
            mod_phase(l)
            if cfg.stop == "mod":
                break
            if NSEQ and cfg.stop != "sonly":
                X.update(NT=NT, Z=zD, BO=boT, XW=xD)
                p1_phase(0, l, xp if l == 0 else xD)
                sch.barrier()
                g1_phase(0, l)
                sch.barrier()
                mix_phase(0, l)
                sch.barrier()
                g2_phase(0, l)
                g3_phase(0, l, xp if l == 0 else xD)
                g4_phase(0, l)
                g5_phase(0, l, yp if last else xD)
            if SLEN:
                TP = min(1024, SLEN)
                NPASS = SLEN // TP
                for p in range(NPASS):
                    sl = slice(p * TP, (p + 1) * TP)
                    X.update(NT=TP // 128, Z=zS.parts[p], BO=boTS[:, :, sl], XW=xSD[sl, :])
                    p1_phase(1, l, xs[sl, :] if l == 0 else xSD[sl, :])
                    sch.barrier()
                    g1_phase(1, l)
                    sch.barrier()
                X.update(NT=SLEN // 128, Z=zS, BO=boTS, XW=xSD)
                mix_phase(1, l, "S")
                sch.barrier()
                if cfg.stop == "smix":
                    break
                for p in range(NPASS):
                    sl = slice(p * TP, (p + 1) * TP)
                    X.update(NT=TP // 128, Z=zS.parts[p], BO=boTS[:, :, sl], XW=xSD[sl, :])
                    g2_phase(1, l)
                    g3_phase(1, l, xs[sl, :] if l == 0 else xSD[sl, :])
                    g4_phase(1, l)
                    g5_phase(1, l, ys[sl, :] if last else xSD[sl, :])

        sch.barrier()
        block = es.enter_context(nc.Block())
        sch.emit(block)
    return nc


_CACHE = {}


def const_tables():
    k = np.arange(128, dtype=np.float32)
    q = np.arange(128, dtype=np.float32)
    rel = q[None, :] - k[:, None]
    Ef = np.maximum(rel, 0.0)
    Lf = (rel >= 0).astype(np.float32)
    Eb = np.maximum(-rel, 0.0)
    Lb = (rel <= 0).astype(np.float32)
    pos = np.stack([k + 1.0, 128.0 - k, 127.0 - k, k], axis=1)
    return np.concatenate([np.eye(128, dtype=np.float32), Ef, Lf, Eb, Lb, pos], axis=1).astype(np.float32)


def rope_tables(slen):
    pos = np.arange(slen)
    rows = (pos // 64).astype(np.float32)
    cols = (pos % 64).astype(np.float32)
    inv = (np.float32(10000.0) ** (-np.arange(0, 64, 2, dtype=np.float32) / np.float32(64.0))).astype(np.float32)
    ar = rows[:, None] * inv[None, :]
    ac = cols[:, None] * inv[None, :]
    C = np.concatenate([np.cos(ar), np.cos(ar), np.cos(ac), np.cos(ac)], axis=1).astype(np.float32)
    S = np.concatenate([-np.sin(ar), np.sin(ar), -np.sin(ac), np.sin(ac)], axis=1).astype(np.float32)
    return np.ascontiguousarray(C), np.ascontiguousarray(S)


def make_in_maps(inputs, cfg, ncores):
    f = lambda a: np.ascontiguousarray(np.asarray(a, dtype=np.float32))
    NSEQ = cfg.nseq
    shared = {
        "w_mod": f(inputs["w_mod"]), "b_mod": f(inputs["b_mod"]), "norm_g": f(inputs["norm_g"]),
        "w_in": f(inputs["w_in"]), "rdl": f(inputs["ret_decay_logit"]).reshape(2, 16),
        "mlp_ng": f(inputs["mlp_norm_g"]), "mlp_ws": f(inputs["mlp_ws"]), "mlp_bs": f(inputs["mlp_bs"]),
        "dlam": f(inputs["diff_lambda"]).reshape(2, 512), "dsub": f(inputs["diff_subln_g"]),
        "w_br": f(inputs["w_branch"]), "w_o": f(inputs["w_o"]), "w_up": f(inputs["w_up"]),
        "w_dn": f(inputs["w_down"]), "ctab": const_tables(),
    }
    if cfg.slen:
        rc, rs_ = rope_tables(cfg.slen)
        shared["ropec"] = rc
        shared["ropes"] = rs_
    xp_all = f(inputs["x_prompt"])
    c = f(inputs["c"])
    cctx = f(inputs["c_ctx"])
    maps = []
    for i in range(ncores):
        b = (i // 4) % 2
        m = dict(shared)
        m["xp"] = np.ascontiguousarray(xp_all[i * NSEQ:(i + 1) * NSEQ].reshape(NSEQ * 256, D))
        m["cond"] = np.ascontiguousarray(np.stack([cctx, c[b]], axis=0))
        if cfg.slen:
            m["xs"] = np.ascontiguousarray(f(inputs["x_sample"][b])[:cfg.slen])
            m["ck"] = np.ascontiguousarray(f(inputs["cache_k"][b]).reshape(2, 512, 2048))
            m["cv"] = np.ascontiguousarray(f(inputs["cache_v"][b]).reshape(2, 512, 2048))
            m["sr"] = f(inputs["state_ret"][b])
        maps.append(m)
    return maps


def kernel(**inputs):
    cfg = Cfg()
    nc = build(cfg)
    maps = make_in_maps(inputs, cfg, 8)
    res = run_bass_kernel_spmd(nc, maps, core_ids=list(range(8)))
    R = res.results
    y_prompt = np.concatenate([r["yp"].reshape(4, 256, D) for r in R], axis=0)
    nck = np.concatenate([r["nck"].reshape(4, 2, 256, 8, 2, 128) for r in R], axis=0)
    ncv = np.concatenate([r["ncv"].reshape(4, 2, 256, 8, 256) for r in R], axis=0)
    nst = np.concatenate([r["nst"] for r in R], axis=0)
    y_sample = np.stack([np.concatenate([R[4 * b + q]["ys"][q * 1024:(q + 1) * 1024] for q in range(4)], axis=0)
                         for b in range(2)], axis=0)
    return (y_prompt, y_sample, nck, ncv, nst)
```

```python
import contextlib
import numpy as np
import concourse.bass as bass
import concourse.mybir as mybir
from concourse.bass_utils import run_bass_kernel_spmd

F32 = mybir.dt.float32
BF16 = mybir.dt.bfloat16
AF = mybir.ActivationFunctionType
ALU = mybir.AluOpType
AX = mybir.AxisListType

D = 4096
KC = 32
NIN = 28672
FH = 11008
O_RQ, O_RK, O_RV, O_RG, O_MU, O_MV, O_DQ, O_DK, O_DV, O_GT = 0, 1024, 2048, 4096, 6144, 8192, 10240, 12288, 14336, 16384
EPS = 1e-6
ENGS = ["pe", "act", "dve", "pool", "sp"]


class Res:
    __slots__ = ("w", "rs")

    def __init__(self):
        self.w = None
        self.rs = []


class Op:
    __slots__ = ("eng", "fn", "waits", "sem", "val", "dma")


class Sched:
    def __init__(self, nc, es, ring_w=8):
        self.nc = nc
        self.ops = {e: [] for e in ENGS}
        self.cnt = {e: 0 for e in ENGS}
        self.dcnt = {e: 0 for e in ENGS}
        self.waited = {e: {} for e in ENGS}
        self.esem = {e: es.enter_context(nc.semaphore("es_" + e)) for e in ENGS}
        self.ring = {e: [es.enter_context(nc.semaphore("dr_%s%d" % (e, i))) for i in range(ring_w)]
                     for e in ("sp", "pool", "act")}
        self.semobj = {}

    def add(self, eng, fn, r=(), w=(), dma=False):
        deps = []
        for x in r:
            if x.w is not None:
                deps.append(x.w)
        for x in w:
            if x.w is not None:
                deps.append(x.w)
            deps.extend(x.rs)
        op = Op()
        op.eng, op.fn, op.dma = eng, fn, dma
        need = {}
        if dma:
            j = self.dcnt[eng]
            self.dcnt[eng] += 1
            ring = self.ring[eng]
            W = len(ring)
            op.sem = ring[j % W]
            op.val = 16 * (j // W + 1)
            if j >= W:
                need[id(op.sem)] = (op.sem, 16 * (j // W))
        else:
            self.cnt[eng] += 1
            op.sem = self.esem[eng]
            op.val = self.cnt[eng]
        for d in deps:
            if d.eng == "pe" and eng == "pe" and not d.dma and not dma:
                continue
            k = id(d.sem)
            if k not in need or need[k][1] < d.val:
                need[k] = (d.sem, d.val)
        waits = []
        wd = self.waited[eng]
        for k, (sem, val) in need.items():
            if wd.get(k, 0) >= val:
                continue
            wd[k] = val
            waits.append((sem, val))
        op.waits = waits
        self.ops[eng].append(op)
        for x in r:
            x.rs.append(op)
        for x in w:
            x.w = op
            x.rs = []
        return op

    def barrier(self):
        targets = []
        for e in ENGS:
            if self.cnt[e] > 0:
                targets.append((self.esem[e], self.cnt[e]))
        for e, ring in self.ring.items():
            n = self.dcnt[e]
            W = len(ring)
            for i in range(min(n, W)):
                cntj = (n - 1 - i) // W + 1
                targets.append((ring[i], 16 * cntj))
        for e in ENGS:
            op = Op()
            op.eng, op.dma = e, False
            op.fn = None
            wd = self.waited[e]
            waits = []
            for sem, val in targets:
                if sem is self.esem[e]:
                    continue
                if wd.get(id(sem), 0) >= val:
                    continue
                wd[id(sem)] = val
                waits.append((sem, val))
            op.waits = waits
            op.sem = None
            op.val = 0
            self.ops[e].append(op)

    def emit(self, block):
        def run(eng, name):
            for op in self.ops[name]:
                for sem, val in op.waits:
                    eng.wait_ge(sem, val)
                if op.fn is None:
                    continue
                inst = op.fn(eng)
                inst.then_inc(op.sem, 16 if op.dma else 1)

        @block.tensor
        def _(e):
            run(e, "pe")

        @block.scalar
        def _(e):
            run(e, "act")

        @block.vector
        def _(e):
            run(e, "dve")

        @block.gpsimd
        def _(e):
            run(e, "pool")

        @block.sync
        def _(e):
            run(e, "sp")


class ZCat:
    def __init__(self, parts, rp):
        self.parts, self.rp = parts, rp

    def __getitem__(self, key):
        rs, cs = key
        p = rs.start // self.rp
        assert (rs.stop - 1) // self.rp == p
        return self.parts[p][rs.start - p * self.rp:rs.stop - p * self.rp, cs]


class Cfg:
    def __init__(self, nseq=4, layers=2, do_sample=False, stop=None, dbg=(), slen=4096):
        self.slen = slen
        self.nseq = nseq
        self.layers = layers
        self.do_sample = do_sample
        self.stop = stop
        self.dbg = dbg


def build(cfg):
    nc = bass.Bass("TRN2", target_bir_lowering=False)
    NSEQ = cfg.nseq
    NT = NSEQ * 2
    T = NT * 128

    def din(name, shape, dt=F32):
        return nc.dram_tensor(name, list(shape), dt, kind="ExternalInput").ap()

    def dout(name, shape, dt=F32):
        return nc.dram_tensor(name, list(shape), dt, kind="ExternalOutput").ap()

    def dscr(name, shape, dt=F32):
        kind = "ExternalOutput" if name in cfg.dbg else "Internal"
        return nc.dram_tensor(name, list(shape), dt, kind=kind).ap()

    xp = din("xp", [T, D])
    cond = din("cond", [2, D])
    w_mod = din("w_mod", [2, D, 6 * D])
    b_mod = din("b_mod", [2, 6 * D])
    norm_g = din("norm_g", [2, 4, D])
    w_in = din("w_in", [2, D, NIN])
    rdl = din("rdl", [2, 16])
    mlp_ng = din("mlp_ng", [2, 2048])
    mlp_ws = din("mlp_ws", [2, 8, 128, 128])
    mlp_bs = din("mlp_bs", [2, 8, 128])
    dlam = din("dlam", [2, 512])
    dsub = din("dsub", [2, 256])
    w_br = din("w_br", [2, 3, 2048, D])
    w_o = din("w_o", [2, D, D])
    w_up = din("w_up", [2, D, 2 * FH])
    w_dn = din("w_dn", [2, FH, D])
    ctab = din("ctab", [128, 128 + 512 + 4])

    yp = dout("yp", [T, D])
    nck = dout("nck", [NSEQ, 2, 256, 2048])
    ncv = dout("ncv", [NSEQ, 2, 256, 2048])
    nst = dout("nst", [NSEQ, 2, 2, 8, 128, 256])

    modD = dscr("modD", [2, 2, 6 * D])
    zD = dscr("zD", [T, NIN])
    xD = dscr("xD", [T, D])
    boT = dscr("boT", [3, 2048, T], BF16)
    TMAX = max(T, min(1024, cfg.slen))
    mgD = dscr("mgD", [TMAX, D])
    m2D = dscr("m2D", [TMAX, D])
    fT = dscr("fT", [FH, TMAX], BF16)

    SLEN = cfg.slen
    PAST = 512
    xs = ck = cv = sr = ropec = ropes = ys = zS = xSD = boTS = kTD = qTD = sbD = None
    if SLEN:
        xs = din("xs", [SLEN, D])
        ck = din("ck", [2, PAST, 2048])
        cv = din("cv", [2, PAST, 2048])
        sr = din("sr", [2, 2, 8, 128, 256])
        ropec = din("ropec", [SLEN, 128])
        ropes = din("ropes", [SLEN, 128])
        ys = dout("ys", [SLEN, D])
        TPS = min(1024, SLEN)
        zS = ZCat([dscr("zS%d" % p_, [TPS, NIN]) for p_ in range(SLEN // TPS)], TPS)
        xSD = dscr("xSD", [SLEN, D])
        boTS = dscr("boTS", [3, 2048, SLEN], BF16)
        kTD = dscr("kTD", [16, 128, PAST + SLEN], BF16)
        qTD = dscr("qTD", [16, 128, SLEN], BF16)
        sbD = dscr("sbD", [SLEN // 128, 128, 2048], BF16)

    es = contextlib.ExitStack()
    with es:
        def sb(name, shape, dt):
            return es.enter_context(nc.sbuf_tensor(name, list(shape), dt))

        BIGA = sb("BIGA", [128, 32768], BF16)
        BIGW = sb("BIGW", [128, 32768], BF16)
        FT = sb("FT", [128, 12288], F32)
        ST = sb("ST", [128, 2048], F32)
        CT = sb("CT", [128, 128 + 512 + 4], F32)
        IDB = sb("IDB", [128, 128], BF16)
        COLS = sb("COLS", [128, 2 * 4 * 32], F32)
        SM = sb("SM", [128, 512], F32)
        PS = es.enter_context(nc.psum_tensor("PS", [128, 8, 512], F32))

        sch = Sched(nc, es)
        A = sch.add
        ident = CT[:, 0:128]
        bank_res = [Res() for _ in range(8)]
        st_res = [Res() for _ in range(4)]
        sm_res = Res()

        def psb(b):
            return PS[:, b, :]

        r_ct = Res()
        A("sp", lambda e: e.dma_start(out=CT[:, :], in_=ctab[:, :]), w=[r_ct], dma=True)
        r_idb = Res()
        A("dve", lambda e: e.tensor_copy(out=IDB[:, :], in_=CT[:, 0:128]), r=[r_ct], w=[r_idb])
        sch.barrier()

        actT = BIGA[:, :].rearrange("p (k t) -> p k t", k=32)
        actT_res = [Res() for _ in range(8)]
        wslot = [BIGW[:, s * 8192:(s + 1) * 8192].rearrange("p (k c) -> p k c", k=16) for s in range(4)]
        wslot_res = [Res() for _ in range(4)]
        state = {"panel": 0, "gb": 0, "evac": 0, "st": 0}

        def gemm(nkc, act_fn, wsrc_fn, npanels, ntiles, m_rows, evac_fn, act_res_fn, gbanks=(0, 1, 2)):
            spp = (nkc + 15) // 16
            for n in range(npanels):
                slots = []
                for s in range(spp):
                    sl = ((state["panel"] % 2) * 2 + s) if spp == 2 else state["panel"] % 4
                    slots.append(sl)
                    k0 = s * 16
                    nk = min(16, nkc - k0)
                    for (c0, c1, src) in wsrc_fn(n, k0, nk):
                        A("pool", (lambda e, sl=sl, nk=nk, c0=c0, c1=c1, src=src:
                                   e.dma_start(out=wslot[sl][:, 0:nk, c0:c1], in_=src)),
                          w=[wslot_res[sl]], dma=True)
                state["panel"] += 1
                for t in range(ntiles):
                    b = gbanks[state["gb"] % len(gbanks)]
                    state["gb"] += 1

                    def mm(e, t=t, b=b, slots=slots, n=n):
                        inst = None
                        for kc in range(nkc):
                            inst = e.matmul(PS[0:m_rows, b, :], act_fn(kc, t, n), wslot[slots[kc // 16]][:, kc % 16, :],
                                            start=(kc == 0), stop=(kc == nkc - 1))
                        return inst
                    A("pe", mm, r=[act_res_fn(t, n)] + [wslot_res[s] for s in slots], w=[bank_res[b]])
                    evac_fn(t, n, b)

        def next_st():
            i = state["st"] % 4
            state["st"] += 1
            return i

        def evac_eng():
            state["evac"] += 1
            return "act" if state["evac"] % 2 == 0 else "dve"

        def copy_op(eng_name, out, in_):
            if eng_name == "act":
                return lambda e: e.activation(out=out, in_=in_, func=AF.Copy)
            return lambda e: e.tensor_copy(out=out, in_=in_)

        def mod_phase(l):
            condf = SM[:, 0:64].rearrange("p (k r) -> p k r", r=2)
            r_c = Res()
            for r_ in range(2):
                def ld(e, r_=r_):
                    with nc.allow_non_contiguous_dma(reason="tiny column-layout load"):
                        return e.dma_start(out=condf[:, :, r_], in_=cond[r_].rearrange("(k p) -> p k", p=128))
                A("sp", ld, w=[r_c], dma=True)
            sT = BIGA[:, 0:64].rearrange("p (k r) -> p k r", r=2)
            r_s = Res()
            A("act", lambda e: e.activation(out=sT, in_=condf, func=AF.Silu), r=[r_c], w=[r_s])

            def wsrc(n, k0, nk):
                return [(0, 512, w_mod[l, k0 * 128:(k0 + nk) * 128, n * 512:(n + 1) * 512]
                         .rearrange("(k p) c -> p k c", p=128))]

            def evac(t, n, b):
                i = next_st()
                bt = ST[0:2, i * 512:(i + 1) * 512]
                A("sp", lambda e: e.dma_start(out=bt, in_=b_mod[l, n * 512:(n + 1) * 512].partition_broadcast(2)),
                  w=[st_res[i]], dma=True)
                A("dve", lambda e: e.tensor_tensor(out=bt, in0=bt, in1=PS[0:2, b, :], op=ALU.add),
                  r=[bank_res[b]], w=[st_res[i]])
                A("sp", lambda e: e.dma_start(out=modD[l, :, n * 512:(n + 1) * 512], in_=bt), r=[st_res[i]], dma=True)

            gemm(32, lambda kc, t, n: sT[:, kc, :], wsrc, 48, 1, 2, evac, lambda t, n: r_s)
            sch.barrier()
            tmp = SM[:, 64:64 + 6 * 32].rearrange("p (v k) -> p v k", v=6)
            r_t = Res()
            for gi in range(2):
                loads = [(0, norm_g[l, 0]), (1, norm_g[l, 2]), (2, modD[l, gi, 0:D]), (3, modD[l, gi, D:2 * D]),
                         (4, modD[l, gi, 3 * D:4 * D]), (5, modD[l, gi, 4 * D:5 * D])]
                for (vi, src) in loads:
                    def ld(e, vi=vi, src=src):
                        with nc.allow_non_contiguous_dma(reason="tiny column-layout load"):
                            return e.dma_start(out=tmp[:, vi, :], in_=src.rearrange("(k p) -> p k", p=128))
                    A("sp", ld, w=[r_t], dma=True)
                cg = COLS[:, gi * 128:(gi + 1) * 128].rearrange("p (v k) -> p v k", v=4)
                r_cols = Res()
                A("dve", lambda e, cg=cg: e.scalar_tensor_tensor(out=cg[:, 0, :], in0=tmp[:, 3, :], scalar=1.0,
                                                                  in1=tmp[:, 0, :], op0=ALU.add, op1=ALU.mult),
                  r=[r_t], w=[r_cols])
                A("dve", lambda e, cg=cg: e.tensor_copy(out=cg[:, 1, :], in_=tmp[:, 2, :]), r=[r_t], w=[r_cols])
                A("dve", lambda e, cg=cg: e.scalar_tensor_tensor(out=cg[:, 2, :], in0=tmp[:, 5, :], scalar=1.0,
                                                                  in1=tmp[:, 1, :], op0=ALU.add, op1=ALU.mult),
                  r=[r_t], w=[r_cols])
                A("dve", lambda e, cg=cg: e.tensor_copy(out=cg[:, 3, :], in_=tmp[:, 4, :]), r=[r_t], w=[r_cols])
                sch.barrier()

        X = {'NT': NT, 'Z': zD, 'BO': boT, 'XW': xD}

        adaln_junk_res = Res()
        def adaln_tile(gi, which, t, xt, r_x):
            cg = COLS[:, gi * 128:(gi + 1) * 128].rearrange("p (v k) -> p v k", v=4)
            Ac, Bc = cg[:, 2 * which, :], cg[:, 2 * which + 1, :]
            junk = BIGW[:, 0:4096]
            ss = SM[:, 300 + t:301 + t]
            r_ss = Res()
            r_junk = adaln_junk_res
            A("act", lambda e: e.activation(out=junk, in_=xt, func=AF.Square, accum_out=ss), r=[r_x], w=[r_junk, r_ss])
            A("dve", lambda e: e.tensor_scalar(out=ss, in0=ss, scalar1=1.0 / D, scalar2=EPS, op0=ALU.mult, op1=ALU.add),
              r=[r_ss], w=[r_ss])
            A("act", lambda e: e.activation(out=ss, in_=ss, func=AF.Sqrt), r=[r_ss], w=[r_ss])
            A("dve", lambda e: e.reciprocal(out=ss, in_=ss), r=[r_ss], w=[r_ss])
            A("dve", lambda e: e.tensor_scalar(out=xt, in0=xt, scalar1=ss, scalar2=None, op0=ALU.mult),
              r=[r_ss, r_x], w=[r_x])
            for q in range(8):
                b = 4 + (state["gb"] % 4)
                state["gb"] += 1

                def tr(e, q=q, b=b):
                    inst = None
                    for j in range(4):
                        c = q * 4 + j
                        inst = e.transpose(PS[:, b, j * 128:(j + 1) * 128], xt[:, c * 128:(c + 1) * 128], ident)
                    return inst
                A("pe", tr, r=[r_x, r_ct], w=[bank_res[b]])
                for j in range(4):
                    c = q * 4 + j
                    o = actT[:, c, t * 128:(t + 1) * 128]
                    i_ = PS[:, b, j * 128:(j + 1) * 128]
                    if j % 2 == 0:
                        A("act", lambda e, o=o, i_=i_, c=c: e.activation(out=o, in_=i_, func=AF.Identity,
                                                                       scale=Ac[:, c:c + 1], bias=Bc[:, c:c + 1]),
                          r=[bank_res[b]], w=[actT_res[t]])
                    else:
                        A("dve", lambda e, o=o, i_=i_, c=c: e.tensor_scalar(out=o, in0=i_, scalar1=Ac[:, c:c + 1],
                                                                           scalar2=Bc[:, c:c + 1], op0=ALU.mult,
                                                                           op1=ALU.add),
                          r=[bank_res[b]], w=[actT_res[t]])

        def p1_phase(gi, l, xsrc):
            Z_, BO_, XW_, NT_ = X['Z'], X['BO'], X['XW'], X['NT']
            xres = [Res(), Res()]
            for t in range(NT_):
                xt = FT[:, (t % 2) * 4096:(t % 2 + 1) * 4096]
                A("sp", lambda e, xt=xt, t=t: e.dma_start(out=xt, in_=xsrc[t * 128:(t + 1) * 128, :]),
                  w=[xres[t % 2]], dma=True)
                adaln_tile(gi, 0, t, xt, xres[t % 2])

        def g1_phase(gi, l):
            Z_, BO_, XW_, NT_ = X['Z'], X['BO'], X['XW'], X['NT']
            def wsrc(n, k0, nk):
                return [(0, 512, w_in[l, k0 * 128:(k0 + nk) * 128, n * 512:(n + 1) * 512]
                         .rearrange("(k p) c -> p k c", p=128))]

            def evac(t, n, b):
                i = next_st()
                st = ST[:, i * 512:(i + 1) * 512]
                en = evac_eng()
                A(en, copy_op(en, st, psb(b)), r=[bank_res[b]], w=[st_res[i]])
                A("sp", lambda e: e.dma_start(out=Z_[t * 128:(t + 1) * 128, n * 512:(n + 1) * 512], in_=st),
                  r=[st_res[i]], dma=True)
                c0 = n * 512
                if gi == 0 and O_DK <= c0 < O_DV:
                    A("sp", lambda e: e.dma_start(out=nck[t // 2, l, (t % 2) * 128:(t % 2 + 1) * 128,
                                                         c0 - O_DK:c0 - O_DK + 512], in_=st),
                      r=[st_res[i]], dma=True)
                if gi == 0 and O_DV <= c0 < O_GT:
                    A("sp", lambda e: e.dma_start(out=ncv[t // 2, l, (t % 2) * 128:(t % 2 + 1) * 128,
                                                         c0 - O_DV:c0 - O_DV + 512], in_=st),
                      r=[st_res[i]], dma=True)

            gemm(32, lambda kc, t, n: actT[:, kc, t * 128:(t + 1) * 128], wsrc, NIN // 512, NT_, 128, evac,
                 lambda t, n: actT_res[t])


        RS = {}

        def R_(name):
            if name not in RS:
                RS[name] = Res()
            return RS[name]

        bctr = {"b": 0}

        def nb():
            b = bctr["b"] % 8
            bctr["b"] += 1
            return b

        def PB(b):
            return PS[:, b, :].bitcast(BF16)

        MASK = sb("MASK", [128, 1024], F32)
        SM2 = sb("SM2", [128, 512], F32)
        SUBB = sb("SUBB", [128, 256], F32)

        def mix_phase(gi, l, mode="P"):
            Z_, BO_, XW_, NT_ = X['Z'], X['BO'], X['XW'], X['NT']
            lam_init = 0.8 - 0.6 * float(np.exp(-0.3 * l))
            s_dk = 128.0 ** -0.5
            a_sc = 128.0 ** -0.5
            Vb = [BIGA[:, c * 2048:(c + 1) * 2048] for c in range(2)]
            QV = [BIGA[:, 4096 + v * 1024:4096 + (v + 1) * 1024] for v in range(6)]
            QT = [[BIGA[:, 10240 + (c * 4 + v) * 1024:10240 + (c * 4 + v + 1) * 1024].rearrange("p (h t) -> p h t", h=8)
                   for v in range(4)] for c in range(2)]
            ATT = BIGA[:, 18432:19456].rearrange("p (h t) -> p h t", h=8)
            UB = [BIGA[:, 19456 + i * 2048:19456 + (i + 1) * 2048].rearrange("p (h v) -> p h v", h=8) for i in range(2)]
            RETO = BIGA[:, 23552:25600]
            MGB = BIGA[:, 25600:29696].bitcast(F32)
            WST = BIGA[:, 29696:30720].rearrange("p (g t) -> p g t", g=8)
            DQb = [BIGW[:, c * 2048:(c + 1) * 2048] for c in range(2)]
            DKb = [BIGW[:, 4096 + c * 2048:4096 + (c + 1) * 2048] for c in range(2)]
            DVb = [BIGW[:, 8192 + c * 2048:8192 + (c + 1) * 2048] for c in range(2)]
            DQT = [BIGW[:, 12288 + c * 2048:12288 + (c + 1) * 2048].rearrange("p (j t) -> p j t", j=16) for c in range(2)]
            DKT = BIGW[:, 16384:20480].rearrange("p (j t) -> p j t", j=16)
            AB = BIGW[:, 20480:20736]
            AT = BIGW[:, 20736:20992].rearrange("p (k t) -> p k t", k=2)
            DIFO = BIGW[:, 20992:23040]
            MLPO = BIGW[:, 23040:25088]
            VN = BIGW[:, 25088:27136]
            BOS = BIGW[:, 27136:29184].rearrange("p (k t) -> p k t", k=16)
            FB = [FT[:, i * 2048:(i + 1) * 2048] for i in range(6)]
            EX = ST[:, 0:512].rearrange("p (j k) -> p j k", j=2)

            lg = SM2[:, 0:16]
            A("sp", lambda e: e.dma_start(out=lg, in_=rdl[l].partition_broadcast(128)), w=[R_("lg")], dma=True)
            A("act", lambda e: e.activation(out=lg, in_=lg, func=AF.Exp, scale=-1.0), r=[R_("lg")], w=[R_("lg")])
            A("dve", lambda e: e.tensor_scalar(out=lg, in0=lg, scalar1=1.0, scalar2=None, op0=ALU.add),
              r=[R_("lg")], w=[R_("lg")])
            A("act", lambda e: e.activation(out=lg, in_=lg, func=AF.Ln), r=[R_("lg")], w=[R_("lg")])
            A("dve", lambda e: e.tensor_scalar(out=lg, in0=lg, scalar1=-1.0, scalar2=None, op0=ALU.mult),
              r=[R_("lg")], w=[R_("lg")])
            DEC = SM2[:, 16:48]
            CD = SM2[:, 48:64]
            for v, (lo, pc) in enumerate([(0, 0), (8, 1), (0, 2), (8, 3)]):
                A("dve", lambda e, v=v, lo=lo, pc=pc: e.tensor_scalar(out=DEC[:, v * 8:(v + 1) * 8], in0=lg[:, lo:lo + 8],
                                                                      scalar1=CT[:, 640 + pc:641 + pc], scalar2=None,
                                                                      op0=ALU.mult),
                  r=[R_("lg"), r_ct], w=[R_("dec")])
            A("act", lambda e: e.activation(out=DEC, in_=DEC, func=AF.Exp), r=[R_("dec")], w=[R_("dec")])
            A("dve", lambda e: e.tensor_scalar(out=DEC[:, 16:32], in0=DEC[:, 16:32], scalar1=s_dk, scalar2=None,
                                               op0=ALU.mult), r=[R_("dec")], w=[R_("dec")])
            A("act", lambda e: e.activation(out=CD, in_=lg, func=AF.Exp, scale=128.0), r=[R_("lg")], w=[R_("cd")])
            MK = MASK[:, :].rearrange("p (h q) -> p h q", h=8)
            Ef, Lf, Eb, Lb = CT[:, 128:256], CT[:, 256:384], CT[:, 384:512], CT[:, 512:640]
            t1, t2 = FB[4][:, 0:128], FB[4][:, 128:256]
            for h in range(8):
                A("act", lambda e, h=h: e.activation(out=t1, in_=Ef, func=AF.Exp, scale=lg[:, h:h + 1]),
                  r=[R_("lg"), r_ct], w=[R_("fb4")])
                A("dve", lambda e: e.scalar_tensor_tensor(out=t1, in0=t1, scalar=s_dk, in1=Lf, op0=ALU.mult, op1=ALU.mult),
                  r=[R_("fb4")], w=[R_("fb4")])
                A("act", lambda e, h=h: e.activation(out=t2, in_=Eb, func=AF.Exp, scale=lg[:, 8 + h:9 + h]),
                  r=[R_("lg"), r_ct], w=[R_("fb4b")])
                A("dve", lambda e: e.scalar_tensor_tensor(out=t2, in0=t2, scalar=s_dk, in1=Lb, op0=ALU.mult, op1=ALU.mult),
                  r=[R_("fb4b")], w=[R_("fb4b")])
                A("dve", lambda e, h=h: e.tensor_tensor(out=MK[:, h, :], in0=t1, in1=t2, op=ALU.add),
                  r=[R_("fb4"), R_("fb4b")], w=[R_("mask")])
            wsf = FB[5][:, 0:1024].rearrange("p (g q) -> p g q", g=8)
            A("sp", lambda e: e.dma_start(out=wsf, in_=mlp_ws[l].rearrange("g p q -> p g q")), w=[R_("fb5")], dma=True)
            for half in range(2):
                b = nb()

                def trw(e, half=half, b=b):
                    inst = None
                    for j in range(4):
                        inst = e.transpose(PS[:, b, j * 128:(j + 1) * 128], wsf[:, half * 4 + j, :], ident)
                    return inst
                A("pe", trw, r=[R_("fb5"), r_ct], w=[bank_res[b]])
                A("dve", lambda e, half=half, b=b: e.tensor_copy(
                    out=WST[:, half * 4:(half + 1) * 4, :], in_=PS[:, b, :].rearrange("p (j t) -> p j t", j=4)),
                  r=[bank_res[b]], w=[R_("wst")])
            BSC = SM2[:, 64:72]

            def ldbs(e):
                with nc.allow_non_contiguous_dma(reason="tiny column-layout load"):
                    return e.dma_start(out=BSC, in_=mlp_bs[l].rearrange("g p -> p g"))
            A("sp", ldbs, w=[R_("bsc")], dma=True)
            A("sp", lambda e: e.dma_start(out=MGB, in_=mlp_ng[l].partition_broadcast(128)), w=[R_("mgb")], dma=True)
            dl = FB[4][:, 512:1024]
            dl4 = dl.rearrange("p (a b d) -> p a b d", a=2, b=2)
            pr = FB[4][:, 1024:1280].rearrange("p (a d) -> p a d", a=2)
            A("sp", lambda e: e.dma_start(out=dl, in_=dlam[l].partition_broadcast(128)), w=[R_("dl")], dma=True)
            A("dve", lambda e: e.tensor_tensor(out=pr, in0=dl4[:, :, 0, :], in1=dl4[:, :, 1, :], op=ALU.mult),
              r=[R_("dl")], w=[R_("pr")])
            LE = SM2[:, 72:74]
            NLAM = SM2[:, 74:75]
            A("dve", lambda e: e.tensor_reduce(out=LE, in_=pr, axis=AX.X, op=ALU.add), r=[R_("pr")], w=[R_("le")])
            A("act", lambda e: e.activation(out=LE, in_=LE, func=AF.Exp), r=[R_("le")], w=[R_("le")])
            A("dve", lambda e: e.tensor_tensor(out=NLAM, in0=LE[:, 1:2], in1=LE[:, 0:1], op=ALU.subtract),
              r=[R_("le")], w=[R_("nlam")])
            A("dve", lambda e: e.tensor_scalar(out=NLAM, in0=NLAM, scalar1=-lam_init, scalar2=None, op0=ALU.add),
              r=[R_("nlam")], w=[R_("nlam")])
            A("sp", lambda e: e.dma_start(out=SUBB[:, :], in_=dsub[l].partition_broadcast(128)), w=[R_("subb")], dma=True)
            A("dve", lambda e: e.tensor_scalar(out=SUBB[:, :], in0=SUBB[:, :], scalar1=1.0 - lam_init, scalar2=None,
                                               op0=ALU.mult), r=[R_("subb")], w=[R_("subb")])

            sch.barrier()

            def rstd_from(ss, n, rname):
                A("dve", lambda e: e.tensor_scalar(out=ss, in0=ss, scalar1=1.0 / n, scalar2=EPS, op0=ALU.mult, op1=ALU.add),
                  r=[R_(rname)], w=[R_(rname)])
                A("act", lambda e: e.activation(out=ss, in_=ss, func=AF.Sqrt), r=[R_(rname)], w=[R_(rname)])
                A("dve", lambda e: e.reciprocal(out=ss, in_=ss), r=[R_(rname)], w=[R_(rname)])

            def store_branch(bi, src, srcname, t):
                for half in range(2):
                    b = nb()

                    def trb(e, half=half, b=b):
                        inst = None
                        for j in range(8):
                            c = half * 8 + j
                            inst = e.transpose(PB(b)[:, j * 128:(j + 1) * 128], src[:, c * 128:(c + 1) * 128], IDB[:, :])
                        return inst
                    A("pe", trb, r=[R_(srcname), r_idb], w=[bank_res[b]])
                    en = evac_eng()
                    A(en, copy_op(en, BOS[:, half * 8:(half + 1) * 8, :], PB(b).rearrange("p (j t) -> p j t", j=8)),
                      r=[bank_res[b]], w=[R_("bos")])
                A("sp", lambda e: e.dma_start(out=BO_[bi, :, t * 128:(t + 1) * 128].rearrange("(k p) t -> p k t", p=128),
                                              in_=BOS), r=[R_("bos")], dma=True)

            def gelu(dst, src, tmp, names):
                rr = [R_(n) for n in names]
                A("act", lambda e: e.activation(out=tmp, in_=src, func=AF.Square), r=[rr[1]], w=[rr[2]])
                A("dve", lambda e: e.tensor_scalar(out=tmp, in0=tmp, scalar1=0.044715, scalar2=1.0, op0=ALU.mult, op1=ALU.add),
                  r=[rr[2]], w=[rr[2]])
                A("dve", lambda e: e.tensor_tensor(out=tmp, in0=tmp, in1=src, op=ALU.mult), r=[rr[1], rr[2]], w=[rr[2]])
                A("act", lambda e: e.activation(out=tmp, in_=tmp, func=AF.Sigmoid, scale=1.5957691216057308),
                  r=[rr[2]], w=[rr[2]])
                A("dve", lambda e: e.tensor_tensor(out=dst, in0=tmp, in1=src, op=ALU.mult), r=[rr[1], rr[2]], w=[rr[0]])

            def gmlp_chunk(t):
                rows = slice(t * 128, (t + 1) * 128)
                MU, MV_, T1, T2 = FB[0][:, :], FB[1][:, :], FB[2][:, :], FB[3][:, :]
                A("sp", lambda e, rows=rows: e.dma_start(out=FB[0][:, :], in_=Z_[rows, O_MU:O_MV]), w=[R_("fb0")], dma=True)
                A("sp", lambda e, rows=rows: e.dma_start(out=FB[1][:, :], in_=Z_[rows, O_MV:O_DQ]), w=[R_("fb1")], dma=True)
                gelu(MU, MU, T1, ["fb0", "fb0", "fb2"])
                gelu(MV_, MV_, T2, ["fb1", "fb1", "fb3"])
                ssv = SM2[:, 200:201]
                A("act", lambda e: e.activation(out=FB[3][:, :], in_=FB[1][:, :], func=AF.Square, accum_out=ssv),
                  r=[R_("fb1")], w=[R_("fb3"), R_("ssv")])
                rstd_from(ssv, 2048.0, "ssv")
                A("dve", lambda e: e.tensor_scalar(out=VN, in0=FB[1][:, :], scalar1=ssv, scalar2=None, op0=ALU.mult),
                  r=[R_("fb1"), R_("ssv")], w=[R_("vn")])
                for gp in range(4):
                    b = nb()

                    def mmg(e, gp=gp, b=b):
                        inst = None
                        for j in range(2):
                            g = gp * 2 + j
                            inst = e.matmul(PS[:, b, j * 256:(j + 1) * 256], WST[:, g, :], VN[:, g * 256:(g + 1) * 256],
                                            start=True, stop=True)
                        return inst
                    A("pe", mmg, r=[R_("wst"), R_("vn")], w=[bank_res[b]])
                    tg = FB[2][:, gp * 512:(gp + 1) * 512]
                    A("dve", lambda e, gp=gp, b=b, tg=tg: e.tensor_tensor(out=tg, in0=PS[:, b, :],
                                                                        in1=MGB[:, gp * 512:(gp + 1) * 512], op=ALU.mult),
                      r=[bank_res[b], R_("mgb")], w=[R_("fb2")])
                    for j in range(2):
                        g = gp * 2 + j
                        A("dve", lambda e, g=g: e.scalar_tensor_tensor(
                            out=MLPO[:, g * 256:(g + 1) * 256], in0=FB[2][:, g * 256:(g + 1) * 256], scalar=BSC[:, g:g + 1],
                            in1=FB[0][:, g * 256:(g + 1) * 256], op0=ALU.add, op1=ALU.mult),
                          r=[R_("fb2"), R_("fb0"), R_("bsc")], w=[R_("mlpo")])
                store_branch(1, MLPO, "mlpo", t)

            for s in range(NSEQ if mode == "P" else 0):
                for c in range(2):
                    t = 2 * s + c
                    rows = slice(t * 128, (t + 1) * 128)
                    rqk = FB[c][:, :]
                    A("sp", lambda e, rqk=rqk, rows=rows: e.dma_start(out=rqk, in_=Z_[rows, O_RQ:O_RV]),
                      w=[R_("fb%d" % c)], dma=True)
                    A("sp", lambda e, c=c, rows=rows: e.dma_start(out=FB[2 + c][:, :], in_=Z_[rows, O_RG:O_MU]),
                      w=[R_("fb%d" % (2 + c))], dma=True)
                    A("pool", lambda e, c=c, rows=rows: e.dma_start(out=Vb[c], in_=Z_[rows, O_RV:O_RG]),
                      w=[R_("vb%d" % c)], dma=True)
                    A("act", lambda e, c=c: e.activation(out=FB[2 + c][:, :], in_=FB[2 + c][:, :], func=AF.Silu),
                      r=[R_("fb%d" % (2 + c))], w=[R_("fb%d" % (2 + c))])
                    q3 = rqk[:, 0:1024].rearrange("p (h d) -> p h d", h=8)
                    k3 = rqk[:, 1024:2048].rearrange("p (h d) -> p h d", h=8)
                    A("dve", lambda e, rqk=rqk: e.tensor_copy(out=QV[0], in_=rqk[:, 0:1024]), r=[R_("fb%d" % c)], w=[R_("qv0")])
                    A("dve", lambda e, rqk=rqk: e.tensor_copy(out=QV[3], in_=rqk[:, 1024:2048]), r=[R_("fb%d" % c)], w=[R_("qv3")])
                    for v, (src3, d0) in enumerate([(q3, 0), (q3, 8), (k3, 16), (k3, 24)]):
                        vi = [1, 2, 4, 5][v]
                        en = "dve"
                        A(en, lambda e, vi=vi, src3=src3, d0=d0: e.tensor_tensor(
                            out=QV[vi].rearrange("p (h d) -> p h d", h=8), in0=src3,
                            in1=DEC[:, d0:d0 + 8].unsqueeze(2).to_broadcast([128, 8, 128]), op=ALU.mult),
                          r=[R_("fb%d" % c), R_("dec")], w=[R_("qv%d" % vi)])
                    for v in range(4):
                        b = nb()

                        def trq(e, v=v, b=b):
                            inst = None
                            for h in range(8):
                                inst = e.transpose(PB(b)[:, h * 128:(h + 1) * 128], QV[v][:, h * 128:(h + 1) * 128], IDB[:, :])
                            return inst
                        A("pe", trq, r=[R_("qv%d" % v), r_idb], w=[bank_res[b]])
                        en = evac_eng()
                        A(en, copy_op(en, QT[c][v], PB(b).rearrange("p (h t) -> p h t", h=8)),
                          r=[bank_res[b]], w=[R_("qt%d%d" % (c, v))])
                    for d_ in range(2):
                        kw = QV[4 + d_]
                        for hp in range(4):
                            b = nb()

                            def mmu(e, hp=hp, b=b, kw=kw, c=c):
                                inst = None
                                for j in range(2):
                                    h = hp * 2 + j
                                    inst = e.matmul(PS[:, b, j * 256:(j + 1) * 256], kw[:, h * 128:(h + 1) * 128],
                                                    Vb[c][:, h * 256:(h + 1) * 256], start=True, stop=True)
                                return inst
                            A("pe", mmu, r=[R_("qv%d" % (4 + d_)), R_("vb%d" % c)], w=[bank_res[b]])
                            ps3 = PS[:, b, :].rearrange("p (j v) -> p j v", j=2)
                            uf = FB[4 + d_][:, :].rearrange("p (h v) -> p h v", h=8)
                            if c == 0:
                                A("dve", lambda e, uf=uf, hp=hp, ps3=ps3: e.tensor_copy(out=uf[:, hp * 2:hp * 2 + 2, :], in_=ps3),
                                  r=[bank_res[b]], w=[R_("uf%d" % d_)])
                                if d_ == 0:
                                    A("act", lambda e, hp=hp, ps3=ps3: e.activation(out=UB[0][:, hp * 2:hp * 2 + 2, :], in_=ps3,
                                                                                  func=AF.Copy),
                                      r=[bank_res[b]], w=[R_("ub0")])
                            else:
                                if d_ == 1:
                                    A("act", lambda e, hp=hp, ps3=ps3: e.activation(out=UB[1][:, hp * 2:hp * 2 + 2, :], in_=ps3,
                                                                                  func=AF.Copy),
                                      r=[bank_res[b]], w=[R_("ub1")])
                                sfin = ST[:, 512 + d_ * 512: 1024 + d_ * 512].rearrange("p (j v) -> p j v", j=2)
                                for j in range(2):
                                    h = hp * 2 + j
                                    if d_ == 0:
                                        A("dve", lambda e, j=j, h=h, uf=uf, ps3=ps3, sfin=sfin: e.scalar_tensor_tensor(
                                            out=sfin[:, j, :], in0=uf[:, h, :], scalar=CD[:, h:h + 1],
                                            in1=ps3[:, j, :], op0=ALU.mult, op1=ALU.add),
                                          r=[bank_res[b], R_("uf0"), R_("cd")], w=[R_("sfin0")])
                                    else:
                                        A("dve", lambda e, j=j, h=h, uf=uf, ps3=ps3, sfin=sfin: e.scalar_tensor_tensor(
                                            out=sfin[:, j, :], in0=ps3[:, j, :], scalar=CD[:, 8 + h:9 + h],
                                            in1=uf[:, h, :], op0=ALU.mult, op1=ALU.add),
                                          r=[bank_res[b], R_("uf1"), R_("cd")], w=[R_("sfin1")])
                                A("sp", lambda e, d_=d_, hp=hp, sfin=sfin, s=s: e.dma_start(
                                    out=nst[s, l, d_, hp * 2:hp * 2 + 2].rearrange("h k v -> k h v"), in_=sfin),
                                  r=[R_("sfin%d" % d_)], dma=True)
                for c in range(2):
                    t = 2 * s + c
                    for half in range(2):
                        b = nb()

                        def mma(e, half=half, b=b, c=c):
                            inst = None
                            for j in range(4):
                                h = half * 4 + j
                                inst = e.matmul(PS[:, b, j * 128:(j + 1) * 128], QT[c][3][:, h, :], QT[c][0][:, h, :],
                                                start=True, stop=True)
                            return inst
                        A("pe", mma, r=[R_("qt%d3" % c), R_("qt%d0" % c)], w=[bank_res[b]])
                        A("dve", lambda e, half=half, b=b: e.tensor_tensor(
                            out=ATT[:, half * 4:(half + 1) * 4, :], in0=PS[:, b, :].rearrange("p (j t) -> p j t", j=4),
                            in1=MK[:, half * 4:(half + 1) * 4, :], op=ALU.mult),
                          r=[bank_res[b], R_("mask")], w=[R_("att")])
                    for hp in range(4):
                        b = nb()

                        def mmo(e, hp=hp, b=b, c=c):
                            inst = None
                            for j in range(2):
                                h = hp * 2 + j
                                o_ = PS[:, b, j * 256:(j + 1) * 256]
                                e.matmul(o_, ATT[:, h, :], Vb[c][:, h * 256:(h + 1) * 256], start=True, stop=False)
                                if c == 1:
                                    inst = e.matmul(o_, QT[c][1][:, h, :], UB[0][:, h, :], start=False, stop=True)
                                else:
                                    inst = e.matmul(o_, QT[c][2][:, h, :], UB[1][:, h, :], start=False, stop=True)
                            return inst
                        A("pe", mmo, r=[R_("att"), R_("vb%d" % c), R_("qt%d1" % c), R_("qt%d2" % c), R_("ub0"), R_("ub1")],
                          w=[bank_res[b]])
                        for j in range(2):
                            h = hp * 2 + j
                            o_ = PS[:, b, j * 256:(j + 1) * 256]
                            st6 = SM2[:, 80 + h * 8:86 + h * 8]
                            mv = SM2[:, 160 + h * 2:162 + h * 2]
                            rn = "hln%d" % h
                            A("dve", lambda e, o_=o_, st6=st6: e.bn_stats(out=st6, in_=o_), r=[bank_res[b]], w=[R_(rn)])
                            A("dve", lambda e, st6=st6, mv=mv: e.bn_aggr(out=mv, in_=st6), r=[R_(rn)], w=[R_(rn)])
                            A("dve", lambda e, mv=mv: e.tensor_scalar(out=mv[:, 1:2], in0=mv[:, 1:2], scalar1=EPS, scalar2=None,
                                                                      op0=ALU.add), r=[R_(rn)], w=[R_(rn)])
                            A("act", lambda e, mv=mv: e.activation(out=mv[:, 1:2], in_=mv[:, 1:2], func=AF.Sqrt), r=[R_(rn)], w=[R_(rn)])
                            A("dve", lambda e, mv=mv: e.reciprocal(out=mv[:, 1:2], in_=mv[:, 1:2]), r=[R_(rn)], w=[R_(rn)])
                            tmpo = ST[:, 1536 + (h % 2) * 256:1792 + (h % 2) * 256]
                            A("dve", lambda e, o_=o_, mv=mv, tmpo=tmpo: e.tensor_scalar(
                                out=tmpo, in0=o_, scalar1=mv[:, 0:1], scalar2=mv[:, 1:2], op0=ALU.subtract, op1=ALU.mult),
                              r=[bank_res[b], R_(rn)], w=[R_("tmpo%d" % (h % 2))])
                            A("dve", lambda e, h=h, c=c, tmpo=tmpo: e.tensor_tensor(
                                out=RETO[:, h * 256:(h + 1) * 256], in0=tmpo, in1=FB[2 + c][:, h * 256:(h + 1) * 256], op=ALU.mult),
                              r=[R_("tmpo%d" % (h % 2)), R_("fb%d" % (2 + c))], w=[R_("reto")])
                    store_branch(0, RETO, "reto", t)
                for c in range(2):
                    gmlp_chunk(2 * s + c)
                for c in range(2):
                    t = 2 * s + c
                    rows = slice(t * 128, (t + 1) * 128)
                    A("pool", lambda e, c=c, rows=rows: e.dma_start(out=DQb[c], in_=Z_[rows, O_DQ:O_DK]), w=[R_("dq%d" % c)], dma=True)
                    A("pool", lambda e, c=c, rows=rows: e.dma_start(out=DKb[c], in_=Z_[rows, O_DK:O_DV]), w=[R_("dk%d" % c)], dma=True)
                    A("pool", lambda e, c=c, rows=rows: e.dma_start(out=DVb[c], in_=Z_[rows, O_DV:O_GT]), w=[R_("dv%d" % c)], dma=True)
                    for which in range(2):
                        for half in range(2):
                            b = nb()
                            src = DQb[c] if which == 0 else DKb[c]

                            def trd(e, half=half, b=b, src=src):
                                inst = None
                                for j in range(8):
                                    cc = half * 8 + j
                                    inst = e.transpose(PB(b)[:, j * 128:(j + 1) * 128], src[:, cc * 128:(cc + 1) * 128], IDB[:, :])
                                return inst
                            A("pe", trd, r=[R_(("dq%d" if which == 0 else "dk%d") % c), r_idb], w=[bank_res[b]])
                            en = evac_eng()
                            if which == 0:
                                dst = DQT[c][:, half * 8:(half + 1) * 8, :]
                                A(en, copy_op(en, dst, PB(b).rearrange("p (j t) -> p j t", j=8)), r=[bank_res[b]], w=[R_("dqt%d" % c)])
                            else:
                                dst = DKT[:, half * 8:(half + 1) * 8, c * 128:(c + 1) * 128]
                                A(en, copy_op(en, dst, PB(b).rearrange("p (j t) -> p j t", j=8)), r=[bank_res[b]], w=[R_("dkt")])
                for c in range(2):
                    t = 2 * s + c
                    for h in range(8):
                        b = nb()

                        def mms(e, h=h, b=b, c=c):
                            inst = None
                            for j in range(2):
                                inst = e.matmul(PS[:, b, j * 256:(j + 1) * 256], DQT[c][:, 2 * h + j, :], DKT[:, 2 * h + j, :],
                                                start=True, stop=True)
                            return inst
                        A("pe", mms, r=[R_("dqt%d" % c), R_("dkt")], w=[bank_res[b]])
                        ps3 = PS[:, b, :].rearrange("p (j k) -> p j k", j=2)
                        mx = SM2[:, 210:212]
                        sm_ = SM2[:, 212:214]
                        A("dve", lambda e, ps3=ps3: e.tensor_reduce(out=mx, in_=ps3, axis=AX.X, op=ALU.max), r=[bank_res[b]], w=[R_("mx")])
                        A("dve", lambda e: e.tensor_scalar(out=mx, in0=mx, scalar1=-a_sc, scalar2=None, op0=ALU.mult),
                          r=[R_("mx")], w=[R_("mx")])
                        for j in range(2):
                            A("act", lambda e, j=j, ps3=ps3: e.activation(out=EX[:, j, :], in_=ps3[:, j, :], func=AF.Exp, scale=a_sc,
                                                                         bias=mx[:, j:j + 1], accum_out=sm_[:, j:j + 1]),
                              r=[bank_res[b], R_("mx")], w=[R_("ex"), R_("sm%d" % j)])
                        A("dve", lambda e: e.reciprocal(out=sm_, in_=sm_), r=[R_("sm0"), R_("sm1")], w=[R_("sm0"), R_("sm1")])
                        A("dve", lambda e: e.tensor_tensor(out=sm_[:, 1:2], in0=sm_[:, 1:2], in1=NLAM, op=ALU.mult),
                          r=[R_("sm1"), R_("nlam")], w=[R_("sm1")])
                        A("dve", lambda e: e.tensor_scalar(out=EX[:, 1, :], in0=EX[:, 1, :], scalar1=sm_[:, 1:2], scalar2=None,
                                                           op0=ALU.mult), r=[R_("ex"), R_("sm1")], w=[R_("ex")])
                        A("dve", lambda e: e.scalar_tensor_tensor(out=AB, in0=EX[:, 0, :], scalar=sm_[:, 0:1], in1=EX[:, 1, :],
                                                                  op0=ALU.mult, op1=ALU.add),
                          r=[R_("ex"), R_("sm0")], w=[R_("ab")])
                        b2 = nb()

                        def tra(e, b2=b2):
                            inst = None
                            for kb in range(2):
                                inst = e.transpose(PB(b2)[:, kb * 128:(kb + 1) * 128], AB[:, kb * 128:(kb + 1) * 128], IDB[:, :])
                            return inst
                        A("pe", tra, r=[R_("ab"), r_idb], w=[bank_res[b2]])
                        A("act", lambda e, b2=b2: e.activation(out=AT, in_=PB(b2)[:, 0:256].rearrange("p (k t) -> p k t", k=2),
                                                              func=AF.Copy), r=[bank_res[b2]], w=[R_("at")])
                        b3 = nb()

                        def mmpv(e, h=h, b3=b3):
                            inst = None
                            for kb in range(2):
                                inst = e.matmul(PS[:, b3, 0:256], AT[:, kb, :], DVb[kb][:, h * 256:(h + 1) * 256],
                                                start=(kb == 0), stop=(kb == 1))
                            return inst
                        A("pe", mmpv, r=[R_("at"), R_("dv0"), R_("dv1")], w=[bank_res[b3]])
                        ssd = SM2[:, 220:221]
                        A("act", lambda e, b3=b3: e.activation(out=EX[:, 0, :], in_=PS[:, b3, 0:256], func=AF.Square, accum_out=ssd),
                          r=[bank_res[b3]], w=[R_("ex"), R_("ssd")])
                        rstd_from(ssd, 256.0, "ssd")
                        A("dve", lambda e, h=h, b3=b3: e.scalar_tensor_tensor(
                            out=DIFO[:, h * 256:(h + 1) * 256], in0=PS[:, b3, 0:256], scalar=ssd, in1=SUBB[:, :],
                            op0=ALU.mult, op1=ALU.mult), r=[bank_res[b3], R_("ssd"), R_("subb")], w=[R_("difo")])
                    store_branch(2, DIFO, "difo", t)

            if mode != "S":
                return
            NCH = SLEN // 128
            NK = PAST + SLEN
            NKB = NK // 128
            NKP = (NK + 511) // 512
            sch.barrier()
            ROPE = ST[:, 0:256]

            def load_rope(i):
                A("sp", lambda e: e.dma_start(out=ST[:, 0:128], in_=ropec[i * 128:(i + 1) * 128, :]), w=[R_("rope")], dma=True)
                A("sp", lambda e: e.dma_start(out=ST[:, 128:256], in_=ropes[i * 128:(i + 1) * 128, :]), w=[R_("rope")], dma=True)

            def rope(dst, src, nh, tA, tB, nsrc, ndst, nA, nB):
                w_ = nh * 128
                src3 = src.rearrange("p (h d) -> p h d", h=nh)
                tA3 = tA[:, 0:w_].rearrange("p (h d) -> p h d", h=nh)
                tB3 = tB[:, 0:w_].rearrange("p (h d) -> p h d", h=nh)
                s4 = src.rearrange("p (m b j) -> p m b j", b=2, j=32)
                t4 = tB[:, 0:w_].rearrange("p (m b j) -> p m b j", b=2, j=32)
                Cb = ST[:, 0:128].unsqueeze(1).to_broadcast([128, nh, 128])
                Sb = ST[:, 128:256].unsqueeze(1).to_broadcast([128, nh, 128])
                A("dve", lambda e: e.tensor_tensor(out=tA3, in0=src3, in1=Cb, op=ALU.mult), r=[R_(nsrc), R_("rope")], w=[R_(nA)])
                A("dve", lambda e: e.tensor_copy(out=t4[:, :, 0, :], in_=s4[:, :, 1, :]), r=[R_(nsrc)], w=[R_(nB)])
                A("act", lambda e: e.activation(out=t4[:, :, 1, :], in_=s4[:, :, 0, :], func=AF.Copy), r=[R_(nsrc)], w=[R_(nB)])
                A("dve", lambda e: e.tensor_tensor(out=tB3, in0=tB3, in1=Sb, op=ALU.mult), r=[R_(nB), R_("rope")], w=[R_(nB)])
                A("dve", lambda e: e.tensor_tensor(out=dst, in0=tA[:, 0:w_], in1=tB[:, 0:w_], op=ALU.add),
                  r=[R_(nA), R_(nB)], w=[R_(ndst)])

            TKS = BIGW[:, 27136:29184].rearrange("p (j t) -> p j t", j=16)

            def transpose16_to(src, nsrc, dram_dst):
                for half in range(2):
                    b = nb()

                    def trk(e, half=half, b=b):
                        inst = None
                        for j in range(8):
                            cc = half * 8 + j
                            inst = e.transpose(PB(b)[:, j * 128:(j + 1) * 128], src[:, cc * 128:(cc + 1) * 128], IDB[:, :])
                        return inst
                    A("pe", trk, r=[R_(nsrc), r_idb], w=[bank_res[b]])
                    en = evac_eng()
                    A(en, copy_op(en, TKS[:, half * 8:(half + 1) * 8, :], PB(b).rearrange("p (j t) -> p j t", j=8)),
                      r=[bank_res[b]], w=[R_("bos")])
                A("sp", lambda e: e.dma_start(out=dram_dst, in_=TKS), r=[R_("bos")], dma=True)

            KB0 = DKb[0]
            for kb in range(PAST // 128):
                A("pool", lambda e, kb=kb: e.dma_start(out=KB0, in_=ck[l, kb * 128:(kb + 1) * 128, :]), w=[R_("dk0")], dma=True)
                transpose16_to(KB0, "dk0", kTD[:, :, kb * 128:(kb + 1) * 128].rearrange("j d t -> d j t"))
            for i in range(NCH):
                rows = slice(i * 128, (i + 1) * 128)
                load_rope(i)
                A("sp", lambda e, rows=rows: e.dma_start(out=FB[0][:, :], in_=Z_[rows, O_DK:O_DV]), w=[R_("fb0")], dma=True)
                rope(KB0, FB[0][:, :], 16, FB[1], FB[2], "fb0", "dk0", "fb1", "fb2")
                transpose16_to(KB0, "dk0", kTD[:, :, PAST + i * 128:PAST + (i + 1) * 128].rearrange("j d t -> d j t"))
                A("sp", lambda e, rows=rows: e.dma_start(out=FB[3][:, :], in_=Z_[rows, O_DQ:O_DK]), w=[R_("fb3")], dma=True)
                rope(DQb[0], FB[3][:, :], 16, FB[4], FB[5], "fb3", "dq0", "fb4", "fb5")
                transpose16_to(DQb[0], "dq0", qTD[:, :, i * 128:(i + 1) * 128].rearrange("j d t -> d j t"))
            sch.barrier()

            SCUR = FB[5][:, :].rearrange("p (h v) -> p h v", h=8)
            WBb = DEC[:, 24:32].unsqueeze(2).to_broadcast([128, 8, 128])
            A("sp", lambda e: e.dma_start(out=SCUR, in_=sr[l, 1].rearrange("h k v -> k h v")), w=[R_("scur")], dma=True)
            for i in range(NCH - 1, -1, -1):
                rows = slice(i * 128, (i + 1) * 128)
                A("act", lambda e: e.activation(out=UB[1], in_=SCUR, func=AF.Copy), r=[R_("scur")], w=[R_("ub1")])
                A("sp", lambda e, i=i: e.dma_start(out=sbD[i].rearrange("k (h v) -> k h v", h=8), in_=UB[1]), r=[R_("ub1")], dma=True)
                load_rope(i)
                A("sp", lambda e, rows=rows: e.dma_start(out=FB[0][:, 0:1024], in_=Z_[rows, O_RK:O_RV]), w=[R_("fb0")], dma=True)
                A("pool", lambda e, rows=rows: e.dma_start(out=Vb[0], in_=Z_[rows, O_RV:O_RG]), w=[R_("vb0")], dma=True)
                rope(FB[0][:, 1024:2048], FB[0][:, 0:1024], 8, FB[1], FB[2], "fb0", "fb0r", "fb1", "fb2")
                A("dve", lambda e: e.tensor_tensor(out=QV[5].rearrange("p (h d) -> p h d", h=8),
                                                   in0=FB[0][:, 1024:2048].rearrange("p (h d) -> p h d", h=8), in1=WBb, op=ALU.mult),
                  r=[R_("fb0r"), R_("dec")], w=[R_("qv5")])
                for hp in range(4):
                    b = nb()

                    def mmub(e, hp=hp, b=b):
                        inst = None
                        for j in range(2):
                            h = hp * 2 + j
                            inst = e.matmul(PS[:, b, j * 256:(j + 1) * 256], QV[5][:, h * 128:(h + 1) * 128],
                                            Vb[0][:, h * 256:(h + 1) * 256], start=True, stop=True)
                        return inst
                    A("pe", mmub, r=[R_("qv5"), R_("vb0")], w=[bank_res[b]])
                    for j in range(2):
                        h = hp * 2 + j
                        A("dve", lambda e, h=h, j=j, b=b: e.scalar_tensor_tensor(
                            out=SCUR[:, h, :], in0=SCUR[:, h, :], scalar=CD[:, 8 + h:9 + h], in1=PS[:, b, j * 256:(j + 1) * 256],
                            op0=ALU.mult, op1=ALU.add), r=[bank_res[b], R_("cd"), R_("ub1")], w=[R_("scur")])
            sch.barrier()

            A("sp", lambda e: e.dma_start(out=SCUR, in_=sr[l, 0].rearrange("h k v -> k h v")), w=[R_("scur")], dma=True)
            for i in range(NCH):
                rows = slice(i * 128, (i + 1) * 128)
                A("act", lambda e: e.activation(out=UB[0], in_=SCUR, func=AF.Copy), r=[R_("scur")], w=[R_("ub0")])
                A("sp", lambda e, i=i: e.dma_start(out=UB[1], in_=sbD[i].rearrange("k (h v) -> k h v", h=8)), w=[R_("ub1")], dma=True)
                load_rope(i)
                A("sp", lambda e, rows=rows: e.dma_start(out=FB[0][:, :], in_=Z_[rows, O_RQ:O_RV]), w=[R_("fb0")], dma=True)
                A("sp", lambda e, rows=rows: e.dma_start(out=FB[4][:, :], in_=Z_[rows, O_RG:O_MU]), w=[R_("fb4")], dma=True)
                A("pool", lambda e, rows=rows: e.dma_start(out=Vb[0], in_=Z_[rows, O_RV:O_RG]), w=[R_("vb0")], dma=True)
                A("act", lambda e: e.activation(out=FB[4][:, :], in_=FB[4][:, :], func=AF.Silu), r=[R_("fb4")], w=[R_("fb4")])
                rope(FB[1][:, :], FB[0][:, :], 16, FB[2], FB[3], "fb0", "fb1", "fb2", "fb3")
                q3 = FB[1][:, 0:1024].rearrange("p (h d) -> p h d", h=8)
                k3 = FB[1][:, 1024:2048].rearrange("p (h d) -> p h d", h=8)
                A("dve", lambda e: e.tensor_copy(out=QV[0], in_=FB[1][:, 0:1024]), r=[R_("fb1")], w=[R_("qv0")])
                A("dve", lambda e: e.tensor_copy(out=QV[3], in_=FB[1][:, 1024:2048]), r=[R_("fb1")], w=[R_("qv3")])
                for v, (src3, d0, vi) in enumerate([(q3, 0, 1), (q3, 8, 2), (k3, 16, 4)]):
                    en = "dve"
                    A(en, lambda e, vi=vi, src3=src3, d0=d0: e.tensor_tensor(
                        out=QV[vi].rearrange("p (h d) -> p h d", h=8), in0=src3,
                        in1=DEC[:, d0:d0 + 8].unsqueeze(2).to_broadcast([128, 8, 128]), op=ALU.mult),
                      r=[R_("fb1"), R_("dec")], w=[R_("qv%d" % vi)])
                for v in range(4):
                    b = nb()

                    def trq(e, v=v, b=b):
                        inst = None
                        for h in range(8):
                            inst = e.transpose(PB(b)[:, h * 128:(h + 1) * 128], QV[v][:, h * 128:(h + 1) * 128], IDB[:, :])
                        return inst
                    A("pe", trq, r=[R_("qv%d" % v), r_idb], w=[bank_res[b]])
                    en = evac_eng()
                    A(en, copy_op(en, QT[0][v], PB(b).rearrange("p (h t) -> p h t", h=8)), r=[bank_res[b]], w=[R_("qt0%d" % v)])
                for half in range(2):
                    b = nb()

                    def mma(e, half=half, b=b):
                        inst = None
                        for j in range(4):
                            h = half * 4 + j
                            inst = e.matmul(PS[:, b, j * 128:(j + 1) * 128], QT[0][3][:, h, :], QT[0][0][:, h, :], start=True, stop=True)
                        return inst
                    A("pe", mma, r=[R_("qt03"), R_("qt00")], w=[bank_res[b]])
                    A("dve", lambda e, half=half, b=b: e.tensor_tensor(
                        out=ATT[:, half * 4:(half + 1) * 4, :], in0=PS[:, b, :].rearrange("p (j t) -> p j t", j=4),
                        in1=MK[:, half * 4:(half + 1) * 4, :], op=ALU.mult), r=[bank_res[b], R_("mask")], w=[R_("att")])
                for hp in range(4):
                    b = nb()

                    def mmo(e, hp=hp, b=b):
                        inst = None
                        for j in range(2):
                            h = hp * 2 + j
                            o_ = PS[:, b, j * 256:(j + 1) * 256]
                            e.matmul(o_, ATT[:, h, :], Vb[0][:, h * 256:(h + 1) * 256], start=True, stop=False)
                            e.matmul(o_, QT[0][1][:, h, :], UB[0][:, h, :], start=False, stop=False)
                            inst = e.matmul(o_, QT[0][2][:, h, :], UB[1][:, h, :], start=False, stop=True)
                        return inst
                    A("pe", mmo, r=[R_("att"), R_("vb0"), R_("qt01"), R_("qt02"), R_("ub0"), R_("ub1")], w=[bank_res[b]])
                    for j in range(2):
                        h = hp * 2 + j
                        o_ = PS[:, b, j * 256:(j + 1) * 256]
                        st6 = SM2[:, 80 + h * 8:86 + h * 8]
                        mv = SM2[:, 160 + h * 2:162 + h * 2]
                        rn = "hln%d" % h
                        A("dve", lambda e, o_=o_, st6=st6: e.bn_stats(out=st6, in_=o_), r=[bank_res[b]], w=[R_(rn)])
                        A("dve", lambda e, st6=st6, mv=mv: e.bn_aggr(out=mv, in_=st6), r=[R_(rn)], w=[R_(rn)])
                        A("dve", lambda e, mv=mv: e.tensor_scalar(out=mv[:, 1:2], in0=mv[:, 1:2], scalar1=EPS, scalar2=None,
                                                                  op0=ALU.add), r=[R_(rn)], w=[R_(rn)])
                        A("act", lambda e, mv=mv: e.activation(out=mv[:, 1:2], in_=mv[:, 1:2], func=AF.Sqrt), r=[R_(rn)], w=[R_(rn)])
                        A("dve", lambda e, mv=mv: e.reciprocal(out=mv[:, 1:2], in_=mv[:, 1:2]), r=[R_(rn)], w=[R_(rn)])
                        tmpo = ST[:, 1536 + (h % 2) * 256:1792 + (h % 2) * 256]
                        A("dve", lambda e, o_=o_, mv=mv, tmpo=tmpo: e.tensor_scalar(
                            out=tmpo, in0=o_, scalar1=mv[:, 0:1], scalar2=mv[:, 1:2], op0=ALU.subtract, op1=ALU.mult),
                          r=[bank_res[b], R_(rn)], w=[R_("tmpo%d" % (h % 2))])
                        A("dve", lambda e, h=h, tmpo=tmpo: e.tensor_tensor(
                            out=RETO[:, h * 256:(h + 1) * 256], in0=tmpo, in1=FB[4][:, h * 256:(h + 1) * 256], op=ALU.mult),
                          r=[R_("tmpo%d" % (h % 2)), R_("fb4")], w=[R_("reto")])
                store_branch(0, RETO, "reto", i)
                for hp in range(4):
                    b = nb()

                    def mmuf(e, hp=hp, b=b):
                        inst = None
                        for j in range(2):
                            h = hp * 2 + j
                            inst = e.matmul(PS[:, b, j * 256:(j + 1) * 256], QV[4][:, h * 128:(h + 1) * 128],
                                            Vb[0][:, h * 256:(h + 1) * 256], start=True, stop=True)
                        return inst
                    A("pe", mmuf, r=[R_("qv4"), R_("vb0")], w=[bank_res[b]])
                    for j in range(2):
                        h = hp * 2 + j
                        A("dve", lambda e, h=h, j=j, b=b: e.scalar_tensor_tensor(
                            out=SCUR[:, h, :], in0=SCUR[:, h, :], scalar=CD[:, h:h + 1], in1=PS[:, b, j * 256:(j + 1) * 256],
                            op0=ALU.mult, op1=ALU.add), r=[bank_res[b], R_("cd"), R_("ub0")], w=[R_("scur")])
            sch.barrier()

            for i in range(NCH):
                gmlp_chunk(i)
            sch.barrier()

            KTH = BIGW[:, 0:2 * NK].rearrange("p (j t) -> p j t", j=2)
            VH = BIGW[:, 9216:9216 + NKB * 256].rearrange("p (k v) -> p k v", k=NKB)
            QTH = BIGW[:, 18432:18432 + 2 * SLEN].rearrange("p (j t) -> p j t", j=2)
            E32 = BIGA[:, 0:4 * NK].bitcast(F32).rearrange("p (j t) -> p j t", j=2)
            ABS = BIGA[:, 18432:18432 + NK]
            ATS = BIGA[:, 23040:23040 + NK].rearrange("p (k t) -> p k t", k=NKB)
            DFH = BIGA[:, 27648:27904]
            DFT = BIGA[:, 27904:28160].rearrange("p (j t) -> p j t", j=2)
            MXP = SM2[:, 224:224 + 2 * NKP].rearrange("p (j k) -> p j k", j=2)
            SMP = SM2[:, 256:256 + 2 * NKP].rearrange("p (j k) -> p j k", j=2)
            mx = SM2[:, 210:212]
            sm_ = SM2[:, 212:214]
            ssd = SM2[:, 220:221]
            for h in range(8):
                A("sp", lambda e, h=h: e.dma_start(out=KTH, in_=kTD[2 * h:2 * h + 2, :, :].rearrange("j d t -> d j t")),
                  w=[R_("kth")], dma=True)
                A("sp", lambda e, h=h: e.dma_start(out=QTH, in_=qTD[2 * h:2 * h + 2, :, :].rearrange("j d t -> d j t")),
                  w=[R_("qth")], dma=True)
                A("pool", lambda e, h=h: e.dma_start(out=VH[:, 0:PAST // 128, :],
                                                     in_=cv[l, :, h * 256:(h + 1) * 256].rearrange("(k p) v -> p k v", p=128)),
                  w=[R_("vh")], dma=True)
                for k0 in range(0, NCH, 8):
                    k1 = min(NCH, k0 + 8)
                    A("pool", lambda e, h=h, k0=k0, k1=k1: e.dma_start(
                        out=VH[:, PAST // 128 + k0:PAST // 128 + k1, :],
                        in_=Z_[k0 * 128:k1 * 128, O_DV + h * 256:O_DV + (h + 1) * 256].rearrange("(k p) v -> p k v", p=128)),
                      w=[R_("vh")], dma=True)
                for i in range(NCH):
                    for ps_ in range(2):
                        for j in range(2):
                            for kp in range(NKP):
                                c0 = kp * 512
                                c1 = min(NK, c0 + 512)
                                b = nb()
                                A("pe", lambda e, b=b, j=j, c0=c0, c1=c1, i=i: e.matmul(
                                    PS[:, b, 0:c1 - c0], QTH[:, j, i * 128:(i + 1) * 128], KTH[:, j, c0:c1], start=True, stop=True),
                                  r=[R_("qth"), R_("kth")], w=[bank_res[b]])
                                if ps_ == 0:
                                    A("dve", lambda e, b=b, j=j, kp=kp, c0=c0, c1=c1: e.tensor_reduce(
                                        out=MXP[:, j, kp:kp + 1], in_=PS[:, b, 0:c1 - c0], axis=AX.X, op=ALU.max),
                                      r=[bank_res[b]], w=[R_("mxp")])
                                else:
                                    A("act", lambda e, b=b, j=j, kp=kp, c0=c0, c1=c1: e.activation(
                                        out=E32[:, j, c0:c1], in_=PS[:, b, 0:c1 - c0], func=AF.Exp, scale=a_sc, bias=mx[:, j:j + 1],
                                        accum_out=SMP[:, j, kp:kp + 1]), r=[bank_res[b], R_("mx")], w=[R_("e32"), R_("smp")])
                        if ps_ == 0:
                            A("dve", lambda e: e.tensor_reduce(out=mx, in_=MXP, axis=AX.X, op=ALU.max), r=[R_("mxp")], w=[R_("mx")])
                            A("dve", lambda e: e.tensor_scalar(out=mx, in0=mx, scalar1=-a_sc, scalar2=None, op0=ALU.mult),
                              r=[R_("mx")], w=[R_("mx")])
                    A("dve", lambda e: e.tensor_reduce(out=sm_, in_=SMP, axis=AX.X, op=ALU.add), r=[R_("smp")], w=[R_("sm")])
                    A("dve", lambda e: e.reciprocal(out=sm_, in_=sm_), r=[R_("sm")], w=[R_("sm")])
                    A("dve", lambda e: e.tensor_tensor(out=sm_[:, 1:2], in0=sm_[:, 1:2], in1=NLAM, op=ALU.mult),
                      r=[R_("sm"), R_("nlam")], w=[R_("sm")])
                    A("dve", lambda e: e.tensor_scalar(out=E32[:, 1, :], in0=E32[:, 1, :], scalar1=sm_[:, 1:2], scalar2=None,
                                                        op0=ALU.mult), r=[R_("e32"), R_("sm")], w=[R_("e32")])
                    A("dve", lambda e: e.scalar_tensor_tensor(out=ABS, in0=E32[:, 0, :], scalar=sm_[:, 0:1], in1=E32[:, 1, :],
                                                              op0=ALU.mult, op1=ALU.add), r=[R_("e32"), R_("sm")], w=[R_("abs")])
                    for g0 in range(0, NKB, 8):
                        g1 = min(NKB, g0 + 8)
                        b = nb()

                        def tra(e, g0=g0, g1=g1, b=b):
                            inst = None
                            for kb in range(g0, g1):
                                inst = e.transpose(PB(b)[:, (kb - g0) * 128:(kb - g0 + 1) * 128], ABS[:, kb * 128:(kb + 1) * 128], IDB[:, :])
                            return inst
                        A("pe", tra, r=[R_("abs"), r_idb], w=[bank_res[b]])
                        en = evac_eng()
                        A(en, copy_op(en, ATS[:, g0:g1, :], PB(b)[:, 0:(g1 - g0) * 128].rearrange("p (k t) -> p k t", k=g1 - g0)),
                          r=[bank_res[b]], w=[R_("ats")])
                    b3 = nb()

                    def mmpv(e, b3=b3):
                        inst = None
                        for kb in range(NKB):
                            inst = e.matmul(PS[:, b3, 0:256], ATS[:, kb, :], VH[:, kb, :], start=(kb == 0), stop=(kb == NKB - 1))
                        return inst
                    A("pe", mmpv, r=[R_("ats"), R_("vh")], w=[bank_res[b3]])
                    A("act", lambda e, b3=b3: e.activation(out=ST[:, 512:768], in_=PS[:, b3, 0:256], func=AF.Square, accum_out=ssd),
                      r=[bank_res[b3]], w=[R_("sq"), R_("ssd")])
                    rstd_from(ssd, 256.0, "ssd")
                    A("dve", lambda e, b3=b3: e.scalar_tensor_tensor(out=DFH, in0=PS[:, b3, 0:256], scalar=ssd, in1=SUBB[:, :],
                                                                     op0=ALU.mult, op1=ALU.mult),
                      r=[bank_res[b3], R_("ssd"), R_("subb")], w=[R_("dfh")])
                    b4 = nb()

                    def trd2(e, b4=b4):
                        inst = None
                        for j in range(2):
                            inst = e.transpose(PB(b4)[:, j * 128:(j + 1) * 128], DFH[:, j * 128:(j + 1) * 128], IDB[:, :])
                        return inst
                    A("pe", trd2, r=[R_("dfh"), r_idb], w=[bank_res[b4]])
                    A("act", lambda e, b4=b4: e.activation(out=DFT, in_=PB(b4)[:, 0:256].rearrange("p (j t) -> p j t", j=2), func=AF.Copy),
                      r=[bank_res[b4]], w=[R_("dft")])
                    A("sp", lambda e, h=h, i=i: e.dma_start(
                        out=BO_[2, h * 256:(h + 1) * 256, i * 128:(i + 1) * 128].rearrange("(j p) t -> p j t", p=128), in_=DFT),
                      r=[R_("dft")], dma=True)


        def g2_phase(gi, l):
            Z_, BO_, XW_, NT_ = X['Z'], X['BO'], X['XW'], X['NT']
            for th in range((NT_ + 3) // 4):
                t0 = th * 4
                ntl = min(4, NT_ - t0)
                bo = BIGA[:, 0:3 * 16 * 512].rearrange("p (b k t) -> p b k t", b=3, k=16)
                r_bo = [Res() for _ in range(3)]
                for bi in range(3):
                    A("sp", lambda e, bi=bi, t0=t0, ntl=ntl: e.dma_start(
                        out=bo[:, bi, :, 0:ntl * 128],
                        in_=BO_[bi, :, t0 * 128:(t0 + ntl) * 128].rearrange("(k p) t -> p k t", p=128)),
                      w=[r_bo[bi]], dma=True)

                def wsrc(n3, k0, nk):
                    n, bi = n3 // 3, n3 % 3
                    return [(0, 512, w_br[l, bi, k0 * 128:(k0 + nk) * 128, n * 512:(n + 1) * 512]
                             .rearrange("(k p) c -> p k c", p=128))]
                acc = [FT[:, i * 512:(i + 1) * 512] for i in range(4)]
                acc_res = [Res() for _ in range(4)]
                gts = [FT[:, 2048 + i * 512:2048 + (i + 1) * 512] for i in range(4)]
                gts_res = [Res() for _ in range(4)]
                tmps = [FT[:, 4096 + i * 512:4096 + (i + 1) * 512] for i in range(4)]
                tmps_res = [Res() for _ in range(4)]
                cnt = {"g": 0}

                def evac(tt, n3, b):
                    n, bi = n3 // 3, n3 % 3
                    t = t0 + tt
                    i = cnt["g"] % 4
                    cnt["g"] += 1
                    gt = gts[i]
                    c0 = O_GT + bi * D + n * 512
                    A("sp", lambda e: e.dma_start(out=gt, in_=Z_[t * 128:(t + 1) * 128, c0:c0 + 512]), w=[gts_res[i]], dma=True)
                    A("act", lambda e: e.activation(out=gt, in_=gt, func=AF.Sigmoid), r=[gts_res[i]], w=[gts_res[i]])
                    if bi == 0:
                        A("dve", lambda e: e.tensor_tensor(out=acc[tt], in0=gt, in1=psb(b), op=ALU.mult),
                          r=[gts_res[i], bank_res[b]], w=[acc_res[tt]])
                    else:
                        A("dve", lambda e: e.tensor_tensor(out=tmps[i], in0=gt, in1=psb(b), op=ALU.mult),
                          r=[gts_res[i], bank_res[b]], w=[tmps_res[i]])
                        A("dve", lambda e: e.tensor_tensor(out=acc[tt], in0=acc[tt], in1=tmps[i], op=ALU.add),
                          r=[tmps_res[i]], w=[acc_res[tt]])
                    if bi == 2:
                        A("sp", lambda e: e.dma_start(out=mgD[t * 128:(t + 1) * 128, n * 512:(n + 1) * 512], in_=acc[tt]),
                          r=[acc_res[tt]], dma=True)

                gemm(16, lambda kc, tt, n3: bo[:, n3 % 3, kc, tt * 128:(tt + 1) * 128], wsrc, 24, ntl, 128, evac,
                     lambda tt, n3: r_bo[n3 % 3])
                sch.barrier()

        def plain_tile(t, xt, r_x):
            for q in range(8):
                b = 4 + (state["gb"] % 4)
                state["gb"] += 1

                def tr(e, q=q, b=b):
                    inst = None
                    for j in range(4):
                        c = q * 4 + j
                        inst = e.transpose(PS[:, b, j * 128:(j + 1) * 128], xt[:, c * 128:(c + 1) * 128], ident)
                    return inst
                A("pe", tr, r=[r_x, r_ct], w=[bank_res[b]])
                for j in range(4):
                    c = q * 4 + j
                    en = "act" if j % 2 == 0 else "dve"
                    A(en, copy_op(en, actT[:, c, t * 128:(t + 1) * 128], PS[:, b, j * 128:(j + 1) * 128]),
                      r=[bank_res[b]], w=[actT_res[t]])

        SSQ = sb("SSQ", [128, 64], F32)
        r_ssq = [Res() for _ in range(8)]
        r_sqj = [Res(), Res()]

        def raw_evac(t, n, b):
            i = next_st()
            st = ST[:, i * 512:(i + 1) * 512]
            A("dve", lambda e: e.tensor_copy(out=st, in_=psb(b)), r=[bank_res[b]], w=[st_res[i]])
            A("act", lambda e: e.activation(out=FT[:, 8192 + (t % 2) * 512:8704 + (t % 2) * 512], in_=st, func=AF.Square,
                                            accum_out=SSQ[:, t * 8 + n:t * 8 + n + 1]),
              r=[st_res[i]], w=[r_ssq[t], r_sqj[t % 2]])
            A("sp", lambda e: e.dma_start(out=m2D[t * 128:(t + 1) * 128, n * 512:(n + 1) * 512], in_=st),
              r=[st_res[i]], dma=True)

        def resid_epilogue(gi, l, gate_off, ng_idx, xsrc, xdst, then_adaln):
            Z_, BO_, XW_, NT_ = X['Z'], X['BO'], X['XW'], X['NT']
            GB = FT[:, 8192:12288]
            NGB = BIGW[:, 8192:16384].bitcast(F32)
            r_gb, r_ngb = Res(), Res()
            A("sp", lambda e: e.dma_start(out=GB, in_=modD[l, gi, gate_off:gate_off + D].partition_broadcast(128)),
              w=[r_gb], dma=True)
            A("sp", lambda e: e.dma_start(out=NGB, in_=norm_g[l, ng_idx].partition_broadcast(128)), w=[r_ngb], dma=True)
            A("dve", lambda e: e.tensor_tensor(out=GB, in0=GB, in1=NGB, op=ALU.mult), r=[r_ngb], w=[r_gb])
            r_m, r_x = Res(), Res()
            for t in range(NT_):
                mt = FT[:, 0:4096]
                xt = FT[:, 4096:8192]
                rows = slice(t * 128, (t + 1) * 128)
                A("sp", lambda e, rows=rows: e.dma_start(out=mt, in_=m2D[rows, :]), w=[r_m], dma=True)
                A("sp", lambda e, rows=rows: e.dma_start(out=xt, in_=xsrc[rows, :]), w=[r_x], dma=True)
                ss = SM[:, 320 + t:321 + t]
                r_ss = Res()
                A("dve", lambda e, t=t, ss=ss: e.tensor_reduce(out=ss, in_=SSQ[:, t * 8:(t + 1) * 8], axis=AX.X, op=ALU.add),
                  r=[r_ssq[t]], w=[r_ss])
                A("dve", lambda e, ss=ss: e.tensor_scalar(out=ss, in0=ss, scalar1=1.0 / D, scalar2=EPS, op0=ALU.mult, op1=ALU.add),
                  r=[r_ss], w=[r_ss])
                A("act", lambda e, ss=ss: e.activation(out=ss, in_=ss, func=AF.Sqrt), r=[r_ss], w=[r_ss])
                A("dve", lambda e, ss=ss: e.reciprocal(out=ss, in_=ss), r=[r_ss], w=[r_ss])
                A("dve", lambda e, ss=ss: e.scalar_tensor_tensor(out=mt, in0=mt, scalar=ss, in1=GB, op0=ALU.mult, op1=ALU.mult),
                  r=[r_ss, r_gb], w=[r_m])
                A("dve", lambda e: e.tensor_tensor(out=xt, in0=xt, in1=mt, op=ALU.add), r=[r_m], w=[r_x])
                A("sp", lambda e, rows=rows: e.dma_start(out=xdst[rows, :], in_=xt), r=[r_x], dma=True)
                if then_adaln:
                    adaln_tile(gi, 1, t, xt, r_x)

        def g3_phase(gi, l, xsrc):
            Z_, BO_, XW_, NT_ = X['Z'], X['BO'], X['XW'], X['NT']
            r_m = [Res(), Res()]
            for t in range(NT_):
                mt = FT[:, (t % 2) * 4096:(t % 2 + 1) * 4096]
                A("sp", lambda e, mt=mt, t=t: e.dma_start(out=mt, in_=mgD[t * 128:(t + 1) * 128, :]), w=[r_m[t % 2]], dma=True)
                plain_tile(t, mt, r_m[t % 2])
            sch.barrier()

            def wsrc(n, k0, nk):
                return [(0, 512, w_o[l, k0 * 128:(k0 + nk) * 128, n * 512:(n + 1) * 512].rearrange("(k p) c -> p k c", p=128))]
            if cfg.stop == "g3a":
                return
            gemm(32, lambda kc, t, n: actT[:, kc, t * 128:(t + 1) * 128], wsrc, 8, NT_, 128, raw_evac, lambda t, n: actT_res[t])
            sch.barrier()
            if cfg.stop == "g3b":
                return
            resid_epilogue(gi, l, 2 * D, 1, xsrc, XW_, True)
            sch.barrier()

        def g4_phase(gi, l):
            Z_, BO_, XW_, NT_ = X['Z'], X['BO'], X['XW'], X['NT']
            def wsrc(n, k0, nk):
                return [(0, 256, w_up[l, k0 * 128:(k0 + nk) * 128, n * 256:(n + 1) * 256].rearrange("(k p) c -> p k c", p=128)),
                        (256, 512, w_up[l, k0 * 128:(k0 + nk) * 128, FH + n * 256:FH + (n + 1) * 256]
                         .rearrange("(k p) c -> p k c", p=128))]
            fa = [FT[:, i * 256:(i + 1) * 256] for i in range(4)]
            fa_res = [Res() for _ in range(4)]
            fb = [BIGW[:, 16384 + i * 256:16384 + (i + 1) * 256] for i in range(4)]
            fb_res = [Res() for _ in range(4)]
            fs = [FT[:, 2048 + i * 128:2048 + (i + 1) * 128].bitcast(BF16).rearrange("p (j t) -> p j t", j=2) for i in range(4)]
            fs_res = [Res() for _ in range(4)]
            cnt = {"i": 0}

            def evac(t, n, b):
                i = cnt["i"] % 4
                cnt["i"] += 1
                A("act", lambda e: e.activation(out=fa[i], in_=PS[:, b, 0:256], func=AF.Silu), r=[bank_res[b]], w=[fa_res[i]])
                fbt = FT[:, 4096 + i * 128:4096 + (i + 1) * 128].bitcast(BF16)
                A("dve", lambda e: e.tensor_tensor(out=fbt, in0=fa[i], in1=PS[:, b, 256:512], op=ALU.mult),
                  r=[fa_res[i], bank_res[b]], w=[fb_res[i]])
                b2 = 4 + (state["gb"] % 4)
                state["gb"] += 1

                def tr(e):
                    inst = None
                    for j in range(2):
                        inst = e.transpose(PB(b2)[:, j * 128:(j + 1) * 128], fbt[:, j * 128:(j + 1) * 128], IDB[:, :])
                    return inst
                A("pe", tr, r=[fb_res[i], r_idb], w=[bank_res[b2]])
                A("pool" if False else "dve", lambda e: e.tensor_copy(out=fs[i], in_=PB(b2)[:, 0:256].rearrange("p (j t) -> p j t", j=2)),
                  r=[bank_res[b2]], w=[fs_res[i]])
                A("sp", lambda e: e.dma_start(out=fT[n * 256:(n + 1) * 256, t * 128:(t + 1) * 128].rearrange("(j p) t -> p j t", p=128),
                                              in_=fs[i]), r=[fs_res[i]], dma=True)

            gemm(32, lambda kc, t, n: actT[:, kc, t * 128:(t + 1) * 128], wsrc, FH // 256, NT_, 128, evac, lambda t, n: actT_res[t])
            sch.barrier()

        def g5_phase(gi, l, xdst):
            Z_, BO_, XW_, NT_ = X['Z'], X['BO'], X['XW'], X['NT']
            nsub = (FH // 128 + 15) // 16
            ah = [BIGA[:, i * 16384:(i + 1) * 16384].rearrange("p (k t) -> p k t", k=16) for i in range(2)]
            ah_res = [Res(), Res()]
            ctr = 0
            for n in range(8):
                for s_ in range(nsub):
                    k0 = s_ * 16
                    nk = min(16, FH // 128 - k0)
                    sl = ctr % 4
                    ai = ctr % 2
                    ctr += 1
                    A("pool", lambda e, sl=sl, nk=nk, k0=k0, n=n: e.dma_start(
                        out=wslot[sl][:, 0:nk, :],
                        in_=w_dn[l, k0 * 128:(k0 + nk) * 128, n * 512:(n + 1) * 512].rearrange("(k p) c -> p k c", p=128)),
                      w=[wslot_res[sl]], dma=True)
                    A("sp", lambda e, ai=ai, nk=nk, k0=k0: e.dma_start(
                        out=ah[ai][:, 0:nk, 0:NT_ * 128], in_=fT[k0 * 128:(k0 + nk) * 128, 0:NT_ * 128].rearrange("(k p) t -> p k t", p=128)),
                      w=[ah_res[ai]], dma=True)
                    for t in range(NT_):
                        def mm(e, t=t, sl=sl, ai=ai, nk=nk, s_=s_):
                            inst = None
                            for kc in range(nk):
                                inst = e.matmul(PS[:, t, :], ah[ai][:, kc, t * 128:(t + 1) * 128], wslot[sl][:, kc, :],
                                                start=(s_ == 0 and kc == 0), stop=(s_ == nsub - 1 and kc == nk - 1))
                            return inst
                        A("pe", mm, r=[ah_res[ai], wslot_res[sl]], w=[bank_res[t]])
                        if s_ == nsub - 1:
                            raw_evac(t, n, t)
            sch.barrier()
            resid_epilogue(gi, l, 5 * D, 3, XW_, xdst, False)
            sch.barrier()

        for l in range(cfg.layers):
            last = (l == cfg.layers - 1)
            mod_phase(l)
            if cfg.stop == "mod":
                break
            if NSEQ and cfg.stop != "sonly":
                X.update(NT=NT, Z=zD, BO=boT, XW=xD)
                p1_phase(0, l, xp if l == 0 else xD)
                sch.barrier()
                g1_phase(0, l)
                sch.barrier()
                mix_phase(0, l)
                sch.barrier()
                g2_phase(0, l)
                g3_phase(0, l, xp if l == 0 else xD)
                g4_phase(0, l)
                g5_phase(0, l, yp if last else xD)
            if SLEN:
                TP = min(1024, SLEN)
                NPASS = SLEN // TP
                for p in range(NPASS):
                    sl = slice(p * TP, (p + 1) * TP)
                    X.update(NT=TP // 128, Z=zS.parts[p], BO=boTS[:, :, sl], XW=xSD[sl, :])
                    p1_phase(1, l, xs[sl, :] if l == 0 else xSD[sl, :])
                    sch.barrier()
                    g1_phase(1, l)
                    sch.barrier()
                X.update(NT=SLEN // 128, Z=zS, BO=boTS, XW=xSD)
                mix_phase(1, l, "S")
                sch.barrier()
                if cfg.stop == "smix":
                    break
                for p in range(NPASS):
                    sl = slice(p * TP, (p + 1) * TP)
                    X.update(NT=TP // 128, Z=zS.parts[p], BO=boTS[:, :, sl], XW=xSD[sl, :])
                    g2_phase(1, l)
                    g3_phase(1, l, xs[sl, :] if l == 0 else xSD[sl, :])
                    g4_phase(1, l)
                    g5_phase(1, l, ys[sl, :] if last else xSD[sl, :])

        sch.barrier()
        block = es.enter_context(nc.Block())
        sch.emit(block)
    return nc


_CACHE = {}


def const_tables():
    k = np.arange(128, dtype=np.float32)
    q = np.arange(128, dtype=np.float32)
    rel = q[None, :] - k[:, None]
    Ef = np.maximum(rel, 0.0)
    Lf = (rel >= 0).astype(np.float32)
    Eb = np.maximum(-rel, 0.0)
    Lb = (rel <= 0).astype(np.float32)
    pos = np.stack([k + 1.0, 128.0 - k, 127.0 - k, k], axis=1)
    return np.concatenate([np.eye(128, dtype=np.float32), Ef, Lf, Eb, Lb, pos], axis=1).astype(np.float32)


def rope_tables(slen):
    pos = np.arange(slen)
    rows = (pos // 64).astype(np.float32)
    cols = (pos % 64).astype(np.float32)
    inv = (np.float32(10000.0) ** (-np.arange(0, 64, 2, dtype=np.float32) / np.float32(64.0))).astype(np.float32)
    ar = rows[:, None] * inv[None, :]
    ac = cols[:, None] * inv[None, :]
    C = np.concatenate([np.cos(ar), np.cos(ar), np.cos(ac), np.cos(ac)], axis=1).astype(np.float32)
    S = np.concatenate([-np.sin(ar), np.sin(ar), -np.sin(ac), np.sin(ac)], axis=1).astype(np.float32)
    return np.ascontiguousarray(C), np.ascontiguousarray(S)


def make_in_maps(inputs, cfg, ncores):
    f = lambda a: np.ascontiguousarray(np.asarray(a, dtype=np.float32))
    NSEQ = cfg.nseq
    shared = {
        "w_mod": f(inputs["w_mod"]), "b_mod": f(inputs["b_mod"]), "norm_g": f(inputs["norm_g"]),
        "w_in": f(inputs["w_in"]), "rdl": f(inputs["ret_decay_logit"]).reshape(2, 16),
        "mlp_ng": f(inputs["mlp_norm_g"]), "mlp_ws": f(inputs["mlp_ws"]), "mlp_bs": f(inputs["mlp_bs"]),
        "dlam": f(inputs["diff_lambda"]).reshape(2, 512), "dsub": f(inputs["diff_subln_g"]),
        "w_br": f(inputs["w_branch"]), "w_o": f(inputs["w_o"]), "w_up": f(inputs["w_up"]),
        "w_dn": f(inputs["w_down"]), "ctab": const_tables(),
    }
    if cfg.slen:
        rc, rs_ = rope_tables(cfg.slen)
        shared["ropec"] = rc
        shared["ropes"] = rs_
    xp_all = f(inputs["x_prompt"])
    c = f(inputs["c"])
    cctx = f(inputs["c_ctx"])
    maps = []
    for i in range(ncores):
        b = (i // 4) % 2
        m = dict(shared)
        m["xp"] = np.ascontiguousarray(xp_all[i * NSEQ:(i + 1) * NSEQ].reshape(NSEQ * 256, D))
        m["cond"] = np.ascontiguousarray(np.stack([cctx, c[b]], axis=0))
        if cfg.slen:
            m["xs"] = np.ascontiguousarray(f(inputs["x_sample"][b])[:cfg.slen])
            m["ck"] = np.ascontiguousarray(f(inputs["cache_k"][b]).reshape(2, 512, 2048))
            m["cv"] = np.ascontiguousarray(f(inputs["cache_v"][b]).reshape(2, 512, 2048))
            m["sr"] = f(inputs["state_ret"][b])
        maps.append(m)
    return maps


def kernel(**inputs):
    cfg = Cfg()
    nc = build(cfg)
    maps = make_in_maps(inputs, cfg, 8)
    res = run_bass_kernel_spmd(nc, maps, core_ids=list(range(8)))
    R = res.results
    y_prompt = np.concatenate([r["yp"].reshape(4, 256, D) for r in R], axis=0)
    nck = np.concatenate([r["nck"].reshape(4, 2, 256, 8, 2, 128) for r in R], axis=0)
    ncv = np.concatenate([r["ncv"].reshape(4, 2, 256, 8, 256) for r in R], axis=0)
    nst = np.concatenate([r["nst"] for r in R], axis=0)
    y_sample = np.stack([np.concatenate([R[4 * b + q]["ys"][q * 1024:(q + 1) * 1024] for q in range(4)], axis=0)
                         for b in range(2)], axis=0)
    return (y_prompt, y_sample, nck, ncv, nst)
```
